# Optimizing a Trainium2 kernel written in Bass

```python
import math
import jax, jax.numpy as jnp
from jax import lax
import numpy as np

D_MODEL = 4096
BATCH = 2
SEQ = 4096
DEPTH = 2
DEC_BATCH = 4
DEC_SEQ = 2048
PAST_LEN = 128

EPS = 1e-6
HEAD_DIM = 128
N_BRANCH = 4
SSD_HEADS = 16
SSD_HEAD_DIM = 64
SSD_INNER = SSD_HEADS * SSD_HEAD_DIM
SSD_GROUPS = 2
SSD_STATE = 128
SSD_CONV = 5
SSD_CHUNK = 128
SSD_CONV_DIM = SSD_INNER + 2 * SSD_GROUPS * SSD_STATE
SWA_HEADS = 8
SWA_KV_HEADS = 2
SWA_WIDTH = SWA_HEADS * HEAD_DIM
SWA_KV_WIDTH = SWA_KV_HEADS * HEAD_DIM
SWA_WINDOW = 128
SWA_BLOCK = 128
ROPE_THETA = 10000.0
S5_WIDTH = 1024
S5_GROUP = 16
S5_GROUPS = S5_WIDTH // S5_GROUP
S5_STATE = 64
NA_HEADS = 8
NA_WIDTH = NA_HEADS * HEAD_DIM
GRID_W = 64
NA_KR = 8
NA_KW = 16
D_FF = ((8 * D_MODEL + 3 * 256 - 1) // (3 * 256)) * 256
IN_SIZES = (SSD_INNER, SSD_CONV_DIM, 2 * SSD_HEADS,
            SWA_WIDTH, SWA_KV_WIDTH, SWA_KV_WIDTH,
            S5_WIDTH,
            NA_WIDTH, NA_WIDTH, NA_WIDTH,
            N_BRANCH * D_MODEL)
N_IN = sum(IN_SIZES)

kernel_name = 'hybrid_ssd_swa_s5_natten_encoder'


def rms_norm(x, g):
    xf = x.astype(jnp.float32)
    y = xf * lax.rsqrt(jnp.mean(xf * xf, axis=-1, keepdims=True) + EPS)
    return (y * g.astype(jnp.float32)).astype(x.dtype)


def rope(x):
    l, d = x.shape[1], x.shape[-1]
    half = d // 2
    inv_freq = ROPE_THETA ** (-jnp.arange(half, dtype=jnp.float32) / half)
    ang = jnp.arange(l, dtype=jnp.float32)[:, None] * inv_freq[None, :]
    cos = jnp.cos(ang)[:, None, :]
    sin = jnp.sin(ang)[:, None, :]
    xf = x.astype(jnp.float32)
    x1, x2 = xf[..., :half], xf[..., half:]
    return jnp.concatenate([x1 * cos - x2 * sin, x2 * cos + x1 * sin], axis=-1).astype(x.dtype)


def segsum(a):
    t = a.shape[-1]
    cs = jnp.cumsum(a, axis=-1)
    diff = cs[..., :, None] - cs[..., None, :]
    mask = jnp.tril(jnp.ones((t, t), dtype=bool))
    return jnp.where(mask, diff, -jnp.inf)


def ssd_scan(x, dt, a, bm, cm):
    b, l, h, p = x.shape
    n = bm.shape[-1]
    nc = l // SSD_CHUNK
    xdt = (x * dt[..., None]).reshape(b, nc, SSD_CHUNK, h, p)
    bc = bm.reshape(b, nc, SSD_CHUNK, h, n)
    cc = cm.reshape(b, nc, SSD_CHUNK, h, n)
    da = jnp.transpose((dt * a).reshape(b, nc, SSD_CHUNK, h), (0, 3, 1, 2))
    a_cs = jnp.cumsum(da, axis=-1)
    scores = jnp.einsum('bclhn,bcshn->bhcls', cc, bc) * jnp.exp(segsum(da))
    y_diag = jnp.einsum('bhcls,bcshp->bclhp', scores, xdt)
    decay_to_end = jnp.exp(a_cs[..., -1:] - a_cs)
    states = jnp.einsum('bclhn,bhcl,bclhp->bchpn', bc, decay_to_end, xdt)
    chunk_decay = jnp.exp(segsum(jnp.pad(a_cs[..., -1], ((0, 0), (0, 0), (1, 0)))))
    states = jnp.concatenate([jnp.zeros_like(states[:, :1]), states], axis=1)
    states = jnp.einsum('bhzc,bchpn->bzhpn', chunk_decay, states)[:, :-1]
    y_off = jnp.einsum('bclhn,bchpn,bhcl->bclhp', cc, states, jnp.exp(a_cs))
    return (y_diag + y_off).reshape(b, l, h, p)


def ssd_mixer(z, xbc, dt_raw, conv_w, conv_b, dt_bias, a_log, d_skip, norm_g):
    b, l, _ = xbc.shape
    pad = SSD_CONV // 2
    xbc = lax.conv_general_dilated(xbc, conv_w[:, None, :], window_strides=(1,),
                                   padding=((pad, pad),), dimension_numbers=('NWC', 'WIO', 'NWC'),
                                   feature_group_count=SSD_CONV_DIM) + conv_b
    xbc = jax.nn.silu(xbc).astype(jnp.float32)
    xs = xbc[..., :SSD_INNER].reshape(b, l, SSD_HEADS, SSD_HEAD_DIM)
    rep = SSD_HEADS // SSD_GROUPS
    bm = jnp.repeat(xbc[..., SSD_INNER:SSD_INNER + SSD_GROUPS * SSD_STATE]
                    .reshape(b, l, SSD_GROUPS, SSD_STATE), rep, axis=2)
    cm = jnp.repeat(xbc[..., SSD_INNER + SSD_GROUPS * SSD_STATE:]
                    .reshape(b, l, SSD_GROUPS, SSD_STATE), rep, axis=2)
    dt = jax.nn.softplus(dt_raw.astype(jnp.float32).reshape(b, l, 2, SSD_HEADS)
                         + dt_bias.astype(jnp.float32))
    a = -jnp.exp(a_log.astype(jnp.float32))
    y_fwd = ssd_scan(xs, dt[:, :, 0], a[0], bm, cm)
    flip = lambda t: jnp.flip(t, axis=1)
    y_bwd = flip(ssd_scan(flip(xs), flip(dt[:, :, 1]), a[1], flip(bm), flip(cm)))
    y = y_fwd + y_bwd + d_skip.astype(jnp.float32)[:, None] * xs
    y = y.reshape(b, l, SSD_INNER) * jax.nn.silu(z.astype(jnp.float32))
    return rms_norm(y, norm_g).astype(z.dtype)


def window_attention(q, k, v, q_norm_g, k_norm_g, sink):
    b, l, _ = q.shape
    nb = l // SWA_BLOCK
    grp = SWA_HEADS // SWA_KV_HEADS
    q = rope(rms_norm(q.reshape(b, l, SWA_HEADS, HEAD_DIM), q_norm_g))
    k = rope(rms_norm(k.reshape(b, l, SWA_KV_HEADS, HEAD_DIM), k_norm_g))
    v = v.reshape(b, l, SWA_KV_HEADS, HEAD_DIM)
    qb = q.reshape(b, nb, SWA_BLOCK, SWA_KV_HEADS, grp, HEAD_DIM)
    pad = ((0, 0), (SWA_BLOCK, SWA_BLOCK), (0, 0), (0, 0))

    def band(t):
        tp = jnp.pad(t, pad).reshape(b, nb + 2, SWA_BLOCK, SWA_KV_HEADS, HEAD_DIM)
        return jnp.concatenate([tp[:, :-2], tp[:, 1:-1], tp[:, 2:]], axis=2)

    kw, vw = band(k), band(v)
    s = jnp.einsum('bnqkgd,bnskd->bnkgqs', qb, kw).astype(jnp.float32) * (HEAD_DIM ** -0.5)
    qpos = jnp.arange(nb)[:, None, None] * SWA_BLOCK + jnp.arange(SWA_BLOCK)[None, :, None]
    kpos = jnp.arange(nb)[:, None, None] * SWA_BLOCK - SWA_BLOCK + jnp.arange(3 * SWA_BLOCK)[None, None, :]
    valid = (jnp.abs(kpos - qpos) <= SWA_WINDOW) & (kpos >= 0) & (kpos < l)
    s = jnp.where(valid[None, :, None, None], s, -jnp.inf)
    sk = sink.astype(jnp.float32).reshape(SWA_KV_HEADS, grp)[None, None, :, :, None, None]
    m = jnp.maximum(jnp.max(s, axis=-1, keepdims=True), sk)
    p = jnp.exp(s - m)
    denom = jnp.sum(p, axis=-1, keepdims=True) + jnp.exp(sk - m)
    o = jnp.einsum('bnkgqs,bnskd->bnqkgd', (p / denom).astype(vw.dtype), vw)
    return o.reshape(b, l, SWA_WIDTH)


def _linear_recurrence(e1, e2):
    a1, b1 = e1
    a2, b2 = e2
    return a1 * a2, a2 * b1 + b2


def s5_mixer(u, a_re, a_im, log_step, b_re, b_im, c_re, c_im, d_skip, glu_w, glu_b):
    b, l, _ = u.shape
    f32 = jnp.float32
    uf = u.astype(f32).reshape(b, l, S5_GROUPS, S5_GROUP)
    lam = lax.complex(a_re.astype(f32), a_im.astype(f32))
    step = jnp.exp(log_step.astype(f32))[..., None]
    lam_bar = jnp.exp(lam * step)
    b_c = lax.complex(b_re.astype(f32), b_im.astype(f32))
    b_bar = ((lam_bar - 1.0) / lam)[..., None] * b_c[None]
    c_c = lax.complex(c_re.astype(f32), c_im.astype(f32))

    def run(direction, reverse):
        bu = jnp.einsum('blgi,gpi->blgp', uf, b_bar[direction])
        a = jnp.broadcast_to(lam_bar[direction], bu.shape)
        _, xs = lax.associative_scan(_linear_recurrence, (a, bu), reverse=reverse, axis=1)
        return jnp.real(jnp.einsum('blgp,gip->blgi', xs, c_c[direction]))

    y = run(0, False) + run(1, True) + d_skip.astype(f32).reshape(S5_GROUPS, S5_GROUP) * uf
    g = jax.nn.gelu(y.reshape(b, l, S5_WIDTH))
    out = g * jax.nn.sigmoid(g @ glu_w.astype(f32) + glu_b.astype(f32))
    return out.astype(u.dtype)


def neighborhood_attention(q, k, v, q_norm_g, k_norm_g, rpb):
    b, l, _ = q.shape
    rows = l // GRID_W
    kr = min(NA_KR, rows)
    shp = (b, rows, GRID_W, NA_HEADS, HEAD_DIM)
    q = rms_norm(q.reshape(b, l, NA_HEADS, HEAD_DIM), q_norm_g).reshape(shp)
    k = rms_norm(k.reshape(b, l, NA_HEADS, HEAD_DIM), k_norm_g).reshape(shp)
    v = v.reshape(shp)
    r = jnp.arange(rows)
    row_start = jnp.clip(r - kr // 2, 0, rows - kr)
    row_idx = row_start[:, None] + jnp.arange(kr)[None, :]
    kg = k[:, row_idx].reshape(b, rows, kr * GRID_W, NA_HEADS, HEAD_DIM)
    vg = v[:, row_idx].reshape(b, rows, kr * GRID_W, NA_HEADS, HEAD_DIM)
    s = jnp.einsum('brqhd,brshd->brhqs', q, kg).astype(jnp.float32) * (HEAD_DIM ** -0.5)
    qc = jnp.arange(GRID_W)
    kc = jnp.arange(GRID_W)
    col_start = jnp.clip(qc - NA_KW // 2, 0, GRID_W - NA_KW)
    col_valid = (kc[None, :] >= col_start[:, None]) & (kc[None, :] < col_start[:, None] + NA_KW)
    dr = row_idx - r[:, None] + (NA_KR - 1)
    dc = jnp.clip(kc[None, :] - qc[:, None], -(NA_KW - 1), NA_KW - 1) + (NA_KW - 1)
    bias = rpb.astype(jnp.float32)[:, dr[:, None, :, None], dc[None, :, None, :]]
    bias = jnp.where(col_valid[None, None, :, None, :], bias, -jnp.inf)
    bias = jnp.transpose(bias, (1, 0, 2, 3, 4)).reshape(rows, NA_HEADS, GRID_W, kr * GRID_W)
    p = jax.nn.softmax(s + bias[None], axis=-1)
    o = jnp.einsum('brhqs,brshd->brqhd', p.astype(vg.dtype), vg)
    return o.reshape(b, l, NA_WIDTH)


def block(x, c, ada_w, ada_b, norm1_g, norm2_g, w_in,
          ssd_conv_w, ssd_conv_b, ssd_dt_bias, ssd_a_log, ssd_d, ssd_norm_g,
          swa_q_norm_g, swa_k_norm_g, swa_sink,
          s5_a_re, s5_a_im, s5_log_step, s5_b_re, s5_b_im, s5_c_re, s5_c_im, s5_d, s5_glu_w, s5_glu_b,
          na_q_norm_g, na_k_norm_g, na_rpb,
          w_branch_ssd, w_branch_swa, w_branch_s5, w_branch_na, w_out,
          ffn_w1, ffn_w3, ffn_w2):
    mod = jax.nn.silu(c) @ ada_w + ada_b
    shift1, scale1, gate1, shift2, scale2, gate2 = jnp.split(mod[:, None, :], 6, axis=-1)
    h = rms_norm(x, norm1_g) * (1.0 + scale1) + shift1
    proj = h @ w_in
    splits = np.cumsum(IN_SIZES)[:-1].tolist()
    (z, xbc, dt_raw, q_swa, k_swa, v_swa, u_s5, q_na, k_na, v_na, gate_logits) = jnp.split(proj, splits, axis=-1)
    y_ssd = ssd_mixer(z, xbc, dt_raw, ssd_conv_w, ssd_conv_b, ssd_dt_bias, ssd_a_log, ssd_d, ssd_norm_g) @ w_branch_ssd
    y_swa = window_attention(q_swa, k_swa, v_swa, swa_q_norm_g, swa_k_norm_g, swa_sink) @ w_branch_swa
    y_s5 = s5_mixer(u_s5, s5_a_re, s5_a_im, s5_log_step, s5_b_re, s5_b_im, s5_c_re, s5_c_im,
                    s5_d, s5_glu_w, s5_glu_b) @ w_branch_s5
    y_na = neighborhood_attention(q_na, k_na, v_na, na_q_norm_g, na_k_norm_g, na_rpb) @ w_branch_na
    g_ssd, g_swa, g_s5, g_na = jnp.split(jax.nn.sigmoid(gate_logits), N_BRANCH, axis=-1)
    merged = g_ssd * y_ssd + g_swa * y_swa + g_s5 * y_s5 + g_na * y_na
    x = x + gate1 * (merged @ w_out)
    h2 = rms_norm(x, norm2_g) * (1.0 + scale2) + shift2
    ffn = (jax.nn.silu(h2 @ ffn_w1) * (h2 @ ffn_w3)) @ ffn_w2
    return x + gate2 * ffn


def setup_inputs(seed: int = 0) -> dict:
    key = jax.random.key(seed)
    ks = iter(jax.random.split(key, 48))
    f32 = jnp.float32
    L = DEPTH

    def nrm(shape, scale):
        return jax.random.normal(next(ks), shape, f32) * scale

    def gain(shape):
        return 1.0 + 0.02 * jax.random.normal(next(ks), shape, f32)

    def unif(shape, lo, hi):
        return jax.random.uniform(next(ks), shape, f32, lo, hi)

    x_prompt = nrm((BATCH, SEQ, D_MODEL), 1.0)
    x_sample = nrm((DEC_BATCH, DEC_SEQ, D_MODEL), 1.0)
    c_prompt = nrm((BATCH, D_MODEL), 1.0)
    c_sample = nrm((DEC_BATCH, D_MODEL), 1.0)
    ada_w = nrm((L, D_MODEL, 6 * D_MODEL), 0.2 * D_MODEL ** -0.5)
    ada_b = nrm((L, 6 * D_MODEL), 0.02)
    norm1_g = gain((L, D_MODEL))
    norm2_g = gain((L, D_MODEL))
    w_in = nrm((L, D_MODEL, N_IN), D_MODEL ** -0.5)
    ssd_conv_w = nrm((L, SSD_CONV, SSD_CONV_DIM), SSD_CONV ** -0.5)
    ssd_conv_b = nrm((L, SSD_CONV_DIM), 0.02)
    dt0 = jnp.exp(unif((L, 2, SSD_HEADS), math.log(1e-3), math.log(1e-1)))
    ssd_dt_bias = dt0 + jnp.log(-jnp.expm1(-dt0))
    ssd_a_log = jnp.log(unif((L, 2, SSD_HEADS), 1.0, 16.0))
    ssd_d = gain((L, SSD_HEADS))
    ssd_norm_g = gain((L, SSD_INNER))
    swa_q_norm_g = gain((L, HEAD_DIM))
    swa_k_norm_g = gain((L, HEAD_DIM))
    swa_sink = nrm((L, SWA_HEADS), 0.5)
    s5_a_re = -0.5 + nrm((L, 2, S5_GROUPS, S5_STATE), 0.01)
    s5_a_im = math.pi * jnp.arange(S5_STATE, dtype=f32) + nrm((L, 2, S5_GROUPS, S5_STATE), 0.01)
    s5_log_step = unif((L, 2, S5_GROUPS), math.log(1e-3), math.log(1e-1))
    s5_b_re = nrm((L, S5_GROUPS, S5_STATE, S5_GROUP), (2 * S5_GROUP) ** -0.5)
    s5_b_im = nrm((L, S5_GROUPS, S5_STATE, S5_GROUP), (2 * S5_GROUP) ** -0.5)
    s5_c_re = nrm((L, 2, S5_GROUPS, S5_GROUP, S5_STATE), (2 * S5_STATE) ** -0.5)
    s5_c_im = nrm((L, 2, S5_GROUPS, S5_GROUP, S5_STATE), (2 * S5_STATE) ** -0.5)
    s5_d = nrm((L, S5_WIDTH), 0.5)
    s5_glu_w = nrm((L, S5_WIDTH, S5_WIDTH), S5_WIDTH ** -0.5)
    s5_glu_b = nrm((L, S5_WIDTH), 0.02)
    na_q_norm_g = gain((L, HEAD_DIM))
    na_k_norm_g = gain((L, HEAD_DIM))
    na_rpb = nrm((L, NA_HEADS, 2 * NA_KR - 1, 2 * NA_KW - 1), 0.02)
    w_branch_ssd = nrm((L, SSD_INNER, D_MODEL), SSD_INNER ** -0.5)
    w_branch_swa = nrm((L, SWA_WIDTH, D_MODEL), SWA_WIDTH ** -0.5)
    w_branch_s5 = nrm((L, S5_WIDTH, D_MODEL), S5_WIDTH ** -0.5)
    w_branch_na = nrm((L, NA_WIDTH, D_MODEL), NA_WIDTH ** -0.5)
    w_out = nrm((L, D_MODEL, D_MODEL), D_MODEL ** -0.5)
    ffn_w1 = nrm((L, D_MODEL, D_FF), D_MODEL ** -0.5)
    ffn_w3 = nrm((L, D_MODEL, D_FF), D_MODEL ** -0.5)
    ffn_w2 = nrm((L, D_FF, D_MODEL), D_FF ** -0.5)
    return {'x_prompt': x_prompt, 'x_sample': x_sample, 'c_prompt': c_prompt, 'c_sample': c_sample,
            'ada_w': ada_w, 'ada_b': ada_b, 'norm1_g': norm1_g, 'norm2_g': norm2_g, 'w_in': w_in,
            'ssd_conv_w': ssd_conv_w, 'ssd_conv_b': ssd_conv_b, 'ssd_dt_bias': ssd_dt_bias,
            'ssd_a_log': ssd_a_log, 'ssd_d': ssd_d, 'ssd_norm_g': ssd_norm_g,
            'swa_q_norm_g': swa_q_norm_g, 'swa_k_norm_g': swa_k_norm_g, 'swa_sink': swa_sink,
            's5_a_re': s5_a_re, 's5_a_im': s5_a_im, 's5_log_step': s5_log_step,
            's5_b_re': s5_b_re, 's5_b_im': s5_b_im, 's5_c_re': s5_c_re, 's5_c_im': s5_c_im,
            's5_d': s5_d, 's5_glu_w': s5_glu_w, 's5_glu_b': s5_glu_b,
            'na_q_norm_g': na_q_norm_g, 'na_k_norm_g': na_k_norm_g, 'na_rpb': na_rpb,
            'w_branch_ssd': w_branch_ssd, 'w_branch_swa': w_branch_swa, 'w_branch_s5': w_branch_s5,
            'w_branch_na': w_branch_na, 'w_out': w_out,
            'ffn_w1': ffn_w1, 'ffn_w3': ffn_w3, 'ffn_w2': ffn_w2}


def reference(x_prompt, x_sample, c_prompt, c_sample, ada_w, ada_b, norm1_g, norm2_g, w_in,
              ssd_conv_w, ssd_conv_b, ssd_dt_bias, ssd_a_log, ssd_d, ssd_norm_g,
              swa_q_norm_g, swa_k_norm_g, swa_sink,
              s5_a_re, s5_a_im, s5_log_step, s5_b_re, s5_b_im, s5_c_re, s5_c_im, s5_d, s5_glu_w, s5_glu_b,
              na_q_norm_g, na_k_norm_g, na_rpb,
              w_branch_ssd, w_branch_swa, w_branch_s5, w_branch_na, w_out,
              ffn_w1, ffn_w3, ffn_w2):
    y_prompt = x_prompt
    y_sample = x_sample
    for i in range(DEPTH):
        layer = (ada_w[i], ada_b[i], norm1_g[i], norm2_g[i], w_in[i],
                 ssd_conv_w[i], ssd_conv_b[i], ssd_dt_bias[i], ssd_a_log[i], ssd_d[i], ssd_norm_g[i],
                 swa_q_norm_g[i], swa_k_norm_g[i], swa_sink[i],
                 s5_a_re[i], s5_a_im[i], s5_log_step[i], s5_b_re[i], s5_b_im[i], s5_c_re[i], s5_c_im[i],
                 s5_d[i], s5_glu_w[i], s5_glu_b[i],
                 na_q_norm_g[i], na_k_norm_g[i], na_rpb[i],
                 w_branch_ssd[i], w_branch_swa[i], w_branch_s5[i], w_branch_na[i], w_out[i],
                 ffn_w1[i], ffn_w3[i], ffn_w2[i])
        y_prompt = block(y_prompt, c_prompt, *layer)
        y_sample = block(y_sample, c_sample, *layer)
    return (y_prompt, y_sample)
```

```python
import numpy as np
from contextlib import ExitStack
import concourse.bass as bass
import concourse.mybir as mybir

F32 = mybir.dt.float32
BF16 = mybir.dt.bfloat16
AF = mybir.ActivationFunctionType
ALU = mybir.AluOpType
AX = mybir.AxisListType


class Buf:
    __slots__ = ("t", "lw", "rd", "name")

    def __init__(self, t, name):
        self.t = t
        self.lw = None
        self.rd = {}
        self.name = name

    def __getitem__(self, idx):
        return self.t[idx]


class Eng:
    def __init__(self, name, e, sem, sid):
        self.name = name
        self.e = e
        self.sem = sem
        self.sid = sid
        self.count = 0
        self.known = {}


class Slot:
    def __init__(self, sid):
        self.sid = sid
        self.uses = 0


class Prog:
    def __init__(self, nc):
        self.nc = nc
        self.root = ExitStack()
        self.sems = []
        self.eng = {}
        for name, e in (("pe", nc.tensor), ("act", nc.scalar), ("dve", nc.vector),
                        ("pool", nc.gpsimd), ("sp", nc.sync)):
            sem = self.root.enter_context(nc.semaphore("s_" + name))
            self.eng[name] = Eng(name, e, sem, len(self.sems))
            self.sems.append(sem)
        self.slots = {}
        self.rr = {}
        for q, n in (("sp", 8), ("pool", 6), ("act", 4)):
            lst = []
            for i in range(n):
                sem = self.root.enter_context(nc.semaphore("d_%s%d" % (q, i)))
                lst.append(Slot(len(self.sems)))
                self.sems.append(sem)
            self.slots[q] = lst
            self.rr[q] = 0
        self.phase = None
        self.uid = 0

    def begin_phase(self):
        self.phase = ExitStack()
        self.stack = [self.phase]

    def end_phase(self):
        self.barrier()
        assert len(self.stack) == 1
        self.phase.close()
        self.phase = None

    def push(self):
        st = ExitStack()
        self.stack.append(st)
        self.phase = st

    def pop(self):
        self.barrier()
        self.stack.pop().close()
        self.phase = self.stack[-1]

    def sb(self, shape, dtype, name=None, persist=False):
        self.uid += 1
        name = "%s_%d" % (name or "t", self.uid)
        st = self.root if persist else self.phase
        t = st.enter_context(self.nc.sbuf_tensor(name, list(shape), dtype))
        return Buf(t, name)

    def ps(self, shape, dtype, name=None):
        self.uid += 1
        name = "%s_%d" % (name or "p", self.uid)
        t = self.root.enter_context(self.nc.psum_tensor(name, list(shape), dtype))
        return Buf(t, name)

    def dram(self, name, shape, dtype, kind="Internal"):
        h = self.nc.dram_tensor(name, list(shape), dtype, kind=kind)
        b = Buf(h.ap(), name)
        return b

    def _wait(self, E, sid, v):
        if E.known.get(sid, 0) >= v:
            return
        E.e.wait_ge(self.sems[sid], v)
        E.known[sid] = v

    def _deps(self, E, reads, writes, skip_self=False):
        need = {}
        for b in reads:
            if b.lw is not None:
                sid, v = b.lw
                if need.get(sid, 0) < v:
                    need[sid] = v
        for b in writes:
            if b.lw is not None:
                sid, v = b.lw
                if need.get(sid, 0) < v:
                    need[sid] = v
            for sid, v in b.rd.items():
                if need.get(sid, 0) < v:
                    need[sid] = v
        for sid, v in need.items():
            if skip_self and sid == E.sid:
                continue
            self._wait(E, sid, v)

    def _mark(self, ev, reads, writes):
        sid, v = ev
        for b in reads:
            if b.rd.get(sid, 0) < v:
                b.rd[sid] = v
        for b in writes:
            b.lw = ev
            b.rd = {}

    def op(self, en, fn, reads=(), writes=(), ms=True):
        E = self.eng[en]
        self._deps(E, reads, writes, skip_self=(en == "pe"))
        ins = fn(E.e)
        if ms:
            E.count += 1
            ins.then_inc(E.sem, 1)
            ev = (E.sid, E.count)
        else:
            assert en == "pe"
            ev = (E.sid, E.count + 1)
        self._mark(ev, reads, writes)
        return ins

    def dma(self, q, out, in_, reads=(), writes=()):
        E = self.eng[q]
        sl = self.slots[q]
        i = self.rr[q]
        self.rr[q] = (i + 1) % len(sl)
        slot = sl[i]
        self._deps(E, reads, writes)
        if slot.uses > 0:
            self._wait(E, slot.sid, 16 * slot.uses)
        E.e.dma_start(out=out, in_=in_).then_inc(self.sems[slot.sid], 16)
        slot.uses += 1
        self._mark((slot.sid, 16 * slot.uses), reads, writes)

    def barrier(self):
        for E in self.eng.values():
            for F in self.eng.values():
                if F is E or F.count == 0:
                    continue
                self._wait(E, F.sid, F.count)
            for lst in self.slots.values():
                for s in lst:
                    if s.uses:
                        self._wait(E, s.sid, 16 * s.uses)

    def finish(self):
        self.barrier()
        if self.phase is not None:
            self.phase.close()
        self.root.close()

    def mm(self, out, lhsT, rhs, start, stop, reads, writes, ms=None, sgc=False):
        if ms is None:
            ms = stop
        return self.op("pe", lambda e: e.matmul(out, lhsT=lhsT, rhs=rhs, start=start, stop=stop,
                                                skip_group_check=sgc), reads, writes, ms=ms)

    def tr(self, out, in_, ident, reads, writes, ms=True):
        return self.op("pe", lambda e: e.transpose(out, in_, ident), reads, writes, ms=ms)

    def act(self, out, in_, func, reads, writes, bias=None, scale=None, en="act"):
        kw = {}
        if bias is not None:
            kw["bias"] = bias
        if scale is not None:
            kw["scale"] = scale
        return self.op(en, lambda e: e.activation(out=out, in_=in_, func=func, **kw), reads, writes)

    def ts(self, en, out, in0, s1, s2, op0, op1, reads, writes, accum_out=None):
        kw = {}
        if op1 is not None:
            kw["op1"] = op1
        if accum_out is not None:
            kw["accum_out"] = accum_out
        return self.op(en, lambda e: e.tensor_scalar(out=out, in0=in0, scalar1=s1, scalar2=s2, op0=op0, **kw),
                       reads, writes)

    def tt(self, en, out, in0, in1, op, reads, writes):
        return self.op(en, lambda e: e.tensor_tensor(out=out, in0=in0, in1=in1, op=op), reads, writes)

    def stt(self, out, in0, scalar, in1, op0, op1, reads, writes, accum_out=None):
        kw = {}
        if accum_out is not None:
            kw["accum_out"] = accum_out
        return self.op("dve", lambda e: e.scalar_tensor_tensor(out=out, in0=in0, scalar=scalar, in1=in1,
                                                              op0=op0, op1=op1, **kw), reads, writes)

    def cp(self, en, out, in_, reads, writes):
        if en == "act":
            return self.op(en, lambda e: e.activation(out=out, in_=in_, func=AF.Copy), reads, writes)
        return self.op(en, lambda e: e.tensor_copy(out=out, in_=in_), reads, writes)


from concourse.bass_utils import run_bass_kernel_spmd

D = 4096
KT = 32
NIN = 24608
DFF = 11008
EPS = 1e-6
NCONST = 900
PI = float(np.pi)
MAGIC = 12582912.0

WSHAPES = {
    'ada_w': (D, 6 * D), 'ada_b': (6 * D,), 'norm1_g': (D,), 'norm2_g': (D,), 'w_in': (D, NIN),
    'ssd_conv_w': (5, 1536), 'ssd_conv_b': (1536,), 'ssd_dt_bias': (2, 16), 'ssd_a_log': (2, 16),
    'ssd_d': (16,), 'ssd_norm_g': (1024,), 'swa_q_norm_g': (128,), 'swa_k_norm_g': (128,), 'swa_sink': (8,),
    's5_a_re': (2, 64, 64), 's5_a_im': (2, 64, 64), 's5_log_step': (2, 64), 's5_b_re': (64, 64, 16),
    's5_b_im': (64, 64, 16), 's5_c_re': (2, 64, 16, 64), 's5_c_im': (2, 64, 16, 64), 's5_d': (1024,),
    's5_glu_w': (1024, 1024), 's5_glu_b': (1024,), 'na_q_norm_g': (128,), 'na_k_norm_g': (128,),
    'na_rpb': (8, 15, 31), 'w_branch_ssd': (1024, D), 'w_branch_swa': (1024, D), 'w_branch_s5': (1024, D),
    'w_branch_na': (1024, D), 'w_out': (D, D), 'ffn_w1': (D, DFF), 'ffn_w3': (D, DFF), 'ffn_w2': (DFF, D),
}


def make_consts():
    c = np.zeros((128, NCONST), np.float32)
    k = np.arange(128)[:, None]
    q = np.arange(128)[None, :]
    c[:, 0:128] = np.eye(128)
    c[:, 128:256] = (k <= q)
    c[:, 256:384] = (k >= q)
    c[:, 384:512] = 1.0
    c[0:64, 512:576] = np.eye(64)[::-1]
    qc = np.arange(64)
    cs = np.clip(qc - 8, 0, 48)
    kc = np.arange(64)[:, None]
    cv = ((kc >= cs[None, :]) & (kc < cs[None, :] + 16)).astype(np.float32)
    c[0:64, 576:640] = cv
    c[64:128, 576:640] = cv
    s = np.arange(128)
    c[:, 640] = s + 1
    c[:, 641] = 128 - s
    c[:, 642] = -(s + 1)
    c[:, 643] = -(128 - s)
    c[:, 644:772] = (s + 1)[None, :]
    c[:, 772:900] = (128 - s)[None, :]
    return c


def make_rope(T):
    half = 64
    inv = 10000.0 ** (-np.arange(half, dtype=np.float32) / half)
    ang = np.arange(T, dtype=np.float32)[:, None] * inv[None, :]
    return np.concatenate([np.cos(ang), np.sin(ang)], axis=1).astype(np.float32)


class Ctx:
    pass


def build(T, NL, dbg=False, stages=None, NSEG=1):
    NB = T // 128
    NS = T // 512
    nc = bass.Bass("TRN2", target_bir_lowering=False)
    p = Prog(nc)
    g = Ctx()
    g.T, g.NB, g.NS, g.NL, g.nc, g.p = T, NB, NS, NL, nc, p
    g.NSEG = NSEG
    g.SB = NB // NSEG
    I = {}
    I['x'] = nc.dram_tensor('x', [T, D], F32, kind="ExternalInput")
    I['c'] = nc.dram_tensor('c', [NSEG * 32, 128], F32, kind="ExternalInput")
    I['link'] = nc.dram_tensor('link', [128, 2], F32, kind="ExternalInput")
    I['consts'] = nc.dram_tensor('consts', [128, NCONST], F32, kind="ExternalInput")
    I['rope'] = nc.dram_tensor('rope', [T, 128], F32, kind="ExternalInput")
    for n, shp in WSHAPES.items():
        I[n] = nc.dram_tensor(n, [2] + list(shp), F32, kind="ExternalInput")
    g.I = I
    g.y = Buf(nc.dram_tensor('y', [T, D], F32, kind="ExternalOutput").ap(), 'y')
    g.HT = p.dram('HT', [D, T], BF16)
    g.PTM = p.dram('PTM', [T, 5632], F32)
    g.PFM = p.dram('PFM', [2688, T], F32)
    g.OT = [p.dram('OT%d' % b, [1024, T], BF16) for b in range(4)]
    g.MT = p.dram('MT', [D, T], BF16)
    g.X1 = p.dram('X1', [T, D], F32)
    g.X2 = p.dram('X2', [T, D], F32)
    g.UT = p.dram('UT', [DFF, T], BF16)
    g.XS = p.dram('XS', [T, 1024], F32)
    g.RR = p.dram('RR', [32, T], F32)
    g.QT = p.dram('QT', [1024, T], BF16)
    g.KTd = p.dram('KTd', [1024, T], BF16)
    g.YF = p.dram('YF', [1024, T], F32)
    g.RPB = p.dram('RPB', [8 * 15 * 31 + 256], F32)
    g.dbg = {}
    if dbg:
        for b in range(4):
            g.dbg['ot%d' % b] = Buf(nc.dram_tensor('dbg_ot%d' % b, [1024, T], BF16, kind="ExternalOutput").ap(), 'dbgot')
        g.dbg['x1'] = Buf(nc.dram_tensor('dbg_x1', [T, D], F32, kind="ExternalOutput").ap(), 'dbgx1')
        g.dbg['ptm'] = Buf(nc.dram_tensor('dbg_ptm', [T, 5632], F32, kind="ExternalOutput").ap(), 'dbgptm')
        g.dbg['pfm'] = Buf(nc.dram_tensor('dbg_pfm', [2688, T], F32, kind="ExternalOutput").ap(), 'dbgpfm')
    g.bank = [p.ps([128, 512], F32, 'bank') for _ in range(8)]
    CT = p.sb([128, NCONST], F32, 'consts', persist=True)
    p.dma("sp", CT[:], I['consts'].ap()[:, :], [], [CT])
    g.CT = CT
    g.ident = CT[:, 0:128]
    g.trile = CT[:, 128:256]
    g.trige = CT[:, 256:384]
    g.ones = CT[:, 384:512]
    CB = p.sb([128, 384], BF16, 'constb', persist=True)
    p.cp("dve", CB[:], CT[:, 0:384], [CT], [CB])
    g.CB = CB
    g.identb = CB[:, 0:128]
    g.modT = [[p.sb([128, 192], F32, 'modT', persist=True) for _ in range(NSEG)] for _ in range(NL)]
    g.G1 = [[p.sb([128, 32], F32, 'G1', persist=True) for _ in range(NSEG)] for _ in range(NL)]
    g.G2 = [[p.sb([128, 32], F32, 'G2', persist=True) for _ in range(NSEG)] for _ in range(NL)]
    g.LK = p.sb([128, 2], F32, 'link', persist=True)
    p.dma("sp", g.LK[:], I['link'].ap()[:, :], [], [g.LK])
    setup_mod(g)
    src = Buf(I['x'].ap(), 'xin')
    for l in range(NL):
        dst = g.y if l == NL - 1 else g.X2
        layer(g, l, src, dst, stages)
        src = g.X2
    p.finish()
    return nc


def evac(p, i, out, in_, reads, writes):
    if i % 2 == 0:
        p.cp("act", out, in_, reads, writes)
    else:
        p.cp("dve", out, in_, reads, writes)


def load_T(g, dst, src_ap, rows, tmp, bank):
    p = g.p
    p.dma("sp", tmp[:rows, :], src_ap, [], [tmp])
    p.tr(bank[:, :rows], tmp[:rows, :], g.ident[:rows, :rows], [tmp, g.CT], [bank])


def setup_mod(g):
    p, I = g.p, g.I
    NSEG = g.NSEG
    p.begin_phase()
    tmp = p.sb([128, 128], F32, 'tmp')
    bk = g.bank[0]
    cT = p.sb([128, 32, NSEG], F32, 'cT')
    for sg in range(NSEG):
        p.dma("sp", tmp[:32, :], I['c'].ap()[sg * 32:(sg + 1) * 32, :], [], [tmp])
        p.act(tmp[:32, :], tmp[:32, :], AF.Silu, [tmp], [tmp])
        p.tr(bk[:, :32], tmp[:32, :], g.ident[:32, :32], [tmp, g.CT], [bk])
        p.cp("dve", cT[:, :, sg], bk[:, :32], [bk], [cT])
    wb = [p.sb([128, 32, 128], F32, 'adaw') for _ in range(3)]
    for l in range(g.NL):
        pm = g.bank[1 + (l % 2)]
        aw = I['ada_w'].ap()[l]
        for j in range(192):
            w = wb[j % 3]
            p.dma("sp", w[:], aw[:, j * 128:(j + 1) * 128].rearrange("(k p) c -> p k c", p=128), [], [w])
            for k in range(32):
                p.mm(pm[:, j * NSEG:(j + 1) * NSEG], w[:, k, :], cT[:, k, :], k == 0, k == 31, [w, cT], [pm])
        ab = I['ada_b'].ap()[l].rearrange("(j p) -> j p", p=128)
        for sg in range(NSEG):
            mt = g.modT[l][sg]
            p.cp("dve", mt[:], pm[:, 0:192 * NSEG].rearrange("p (j s) -> p j s", s=NSEG)[:, :, sg], [pm], [mt])
            load_T(g, None, ab[0:128, :], 128, tmp, bk)
            p.tt("dve", mt[:, 0:128], mt[:, 0:128], bk[:, 0:128], ALU.add, [mt, bk], [mt])
            load_T(g, None, ab[128:192, :], 64, tmp, bk)
            p.tt("dve", mt[:, 128:192], mt[:, 128:192], bk[:, 0:64], ALU.add, [mt, bk], [mt])
            for nm, lst, sc in (('norm1_g', g.G1, 1), ('norm2_g', g.G2, 4)):
                load_T(g, None, I[nm].ap()[l].rearrange("(k p) -> k p", p=128), 32, tmp, bk)
                G = lst[l][sg]
                p.stt(G[:], mt[:, sc * 32:(sc + 1) * 32], 1.0, bk[:, :32], ALU.add, ALU.mult, [mt, bk], [G])
    p.end_phase()


def gate_bc(g, dst, col):
    p = g.p
    gl = [p.sb([128, 128], F32, 'gl') for _ in range(2)]
    for k in range(32):
        t = gl[k % 2]
        p.ts("dve", t[:], g.ones, col[1][:, col[2] + k:col[2] + k + 1], None, ALU.mult, None, [g.CT, col[0]], [t])
        bk = g.bank[(k // 4) % 2]
        p.mm(bk[:, (k % 4) * 128:(k % 4 + 1) * 128], t[:], g.ident, True, True, [t, g.CT], [bk])
        if k % 4 == 3:
            evac(p, k // 4, dst[:, (k // 4) * 512:(k // 4 + 1) * 512], bk[:, :], [bk], [dst])


def norm_phase(g, src, Gs, mts, shc):
    p = g.p
    p.begin_phase()
    xt = [p.sb([128, D], F32, 'xt') for _ in range(2)]
    junk = p.sb([128, D], BF16, 'junk')
    hs = [p.sb([128, 32, 128], BF16, 'hs') for _ in range(2)]
    st = [p.sb([128, 4], F32, 'st') for _ in range(2)]
    for tb in range(g.NB):
        x = xt[tb % 2]
        s = st[tb % 2]
        h = hs[tb % 2]
        G = Gs[tb // g.SB]
        mt = mts[tb // g.SB]
        p.dma("sp", x[:], src[tb * 128:(tb + 1) * 128, :], [src], [x])
        p.stt(junk[:], x[:], 1.0, x[:], ALU.mult, ALU.mult, [x], [junk, s], accum_out=s[:, 0:1])
        p.ts("dve", s[:, 1:2], s[:, 0:1], 1.0 / D, EPS, ALU.mult, ALU.add, [s], [s])
        p.act(s[:, 2:3], s[:, 1:2], AF.Sqrt, [s], [s])
        p.op("dve", lambda e: e.reciprocal(out=s[:, 3:4], in_=s[:, 2:3]), [s], [s])
        p.act(x[:], x[:], AF.Copy, [x, s], [x], scale=s[:, 3:4])
        for k in range(32):
            bk = g.bank[(k // 4) % 4]
            p.tr(bk[:, (k % 4) * 128:(k % 4 + 1) * 128], x[:, k * 128:(k + 1) * 128], g.ident, [x, g.CT], [bk])
            if k % 4 == 3:
                for kk in range(k - 3, k + 1):
                    src_ps = bk[:, (kk % 4) * 128:(kk % 4 + 1) * 128]
                    if kk % 2 == 0:
                        p.act(h[:, kk, :], src_ps, AF.Identity, [bk, G, mt], [h], bias=mt[:, shc * 32 + kk:shc * 32 + kk + 1],
                              scale=G[:, kk:kk + 1])
                    else:
                        p.ts("dve", h[:, kk, :], src_ps, G[:, kk:kk + 1], mt[:, shc * 32 + kk:shc * 32 + kk + 1],
                             ALU.mult, ALU.add, [bk, G, mt], [h])
        p.dma("sp", g.HT[:, tb * 128:(tb + 1) * 128].rearrange("(k p) t -> p k t", p=128), h[:], [h], [g.HT])
    p.end_phase()


class WStream:
    def __init__(self, p, n=3):
        self.bufs = [p.sb([128, 16, 512], BF16, 'wb') for _ in range(n)]
        self.i = 0

    def nxt(self):
        b = self.bufs[self.i % len(self.bufs)]
        self.i += 1
        return b


def gemm_chunk(g, ws, AT, KTn, W, c0, w, mode, banks):
    p = g.p
    npc = (KTn + 15) // 16
    for pc in range(npc):
        k0 = pc * 16
        kn = min(16, KTn - k0)
        wb = ws.nxt()
        p.dma("pool", wb[:, :kn, :w], W[k0 * 128:(k0 + kn) * 128, c0:c0 + w].rearrange("(k p) c -> p k c", p=128),
              [], [wb])
        if mode == 'tm':
            for tb in range(4):
                for k in range(kn):
                    kk = k0 + k
                    last = (kk == KTn - 1)
                    p.mm(banks[tb][:, :w], AT[:, kk, tb * 128:(tb + 1) * 128], wb[:, k, :w], kk == 0, last,
                         [AT, wb], [banks[tb]], ms=(last or (tb == 3 and k == kn - 1)))
        else:
            nct = (w + 127) // 128
            for ct in range(nct):
                cw = min(128, w - ct * 128)
                for k in range(kn):
                    kk = k0 + k
                    last = (kk == KTn - 1)
                    p.mm(banks[ct][:cw, :], wb[:, k, ct * 128:ct * 128 + cw], AT[:, kk, :], kk == 0, last,
                         [AT, wb], [banks[ct]], ms=(last or (ct == nct - 1 and k == kn - 1)))


def load_AT(g, dst, srcT, KTn, s):
    g.p.dma("sp", dst[:, :KTn, :], srcT[:KTn * 128, s * 512:(s + 1) * 512].rearrange("(k p) t -> p k t", p=128),
            [srcT], [dst])


def inproj_phase(g, l):
    p = g.p
    p.begin_phase()
    W = g.I['w_in'].ap()[l]
    ws = WStream(p)
    ATs = [p.sb([128, 32, 512], BF16, 'AT') for _ in range(2)]
    stg = [p.sb([128, 512], F32, 'stg') for _ in range(4)]
    si = 0
    jobs = []
    for (c0, n, d0) in ((0, 1024, 0), (2592, 1536, 1024), (5152, 3072, 2560)):
        for o in range(0, n, 512):
            jobs.append((c0 + o, 512, 'tm', d0 + o))
    for (c0, n, d0) in ((1024, 1536, 0), (2560, 32, 1536), (4128, 1024, 1664)):
        for o in range(0, n, 512):
            jobs.append((c0 + o, min(512, n - o), 'fm', d0 + o))
    bi = 0
    for s in range(g.NS):
        AT = ATs[s % 2]
        load_AT(g, AT, g.HT, 32, s)
        for (c0, w, mode, d0) in jobs:
            banks = g.bank[bi * 4:(bi + 1) * 4]
            bi ^= 1
            gemm_chunk(g, ws, AT, 32, W, c0, w, mode, banks)
            if mode == 'tm':
                for tb in range(4):
                    sg = stg[si % 4]
                    evac(p, si, sg[:, :w], banks[tb][:, :w], [banks[tb]], [sg])
                    si += 1
                    p.dma("sp", g.PTM[(s * 4 + tb) * 128:(s * 4 + tb + 1) * 128, d0:d0 + w], sg[:, :w], [sg], [g.PTM])
            else:
                for ct in range((w + 127) // 128):
                    cw = min(128, w - ct * 128)
                    sg = stg[si % 4]
                    evac(p, si, sg[:cw, :], banks[ct][:cw, :], [banks[ct]], [sg])
                    si += 1
                    p.dma("sp", g.PFM[d0 + ct * 128:d0 + ct * 128 + cw, s * 512:(s + 1) * 512], sg[:cw, :], [sg], [g.PFM])
    p.end_phase()
    if 'ptm' in g.dbg and l == 0:
        p.dma("sp", g.dbg['ptm'][:, :], g.PTM[:, :], [g.PTM], [g.dbg['ptm']])
        p.dma("sp", g.dbg['pfm'][:, :], g.PFM[:, :], [g.PFM], [g.dbg['pfm']])


def merge_phase(g, l):
    p = g.p
    p.begin_phase()
    W = g.I['w_in'].ap()[l]
    WB = [g.I[n].ap()[l] for n in ('w_branch_ssd', 'w_branch_swa', 'w_branch_s5', 'w_branch_na')]
    ws = WStream(p)
    ATs = [p.sb([128, 32, 512], BF16, 'AT') for _ in range(2)]
    OTs = [[p.sb([128, 8, 512], BF16, 'OTs') for _ in range(4)] for _ in range(1)]
    acc = [p.sb([128, 512], F32, 'acc') for _ in range(4)]
    sig = [p.sb([128, 512], F32, 'sig') for _ in range(2)]
    outb = [p.sb([128, 512], BF16, 'outb') for _ in range(2)]
    n = 0
    for s in range(g.NS):
        AT = ATs[s % 2]
        load_AT(g, AT, g.HT, 32, s)
        for b in range(4):
            load_AT(g, OTs[0][b], g.OT[b], 8, s)
        for cc in range(8):
            for b in range(4):
                gb = g.bank[0:4]
                yb = g.bank[4:8]
                gemm_chunk(g, ws, AT, 32, W, 8224 + b * D + cc * 512, 512, 'fm', gb)
                gemm_chunk(g, ws, OTs[0][b], 8, WB[b], cc * 512, 512, 'fm', yb)
                for ct in range(4):
                    sg = sig[n % 2]
                    n += 1
                    p.act(sg[:], gb[ct][:, :], AF.Sigmoid, [gb[ct]], [sg])
                    if b == 0:
                        p.tt("dve", acc[ct][:], sg[:], yb[ct][:, :], ALU.mult, [sg, yb[ct]], [acc[ct]])
                    else:
                        p.tt("dve", sg[:], sg[:], yb[ct][:, :], ALU.mult, [sg, yb[ct]], [sg])
                        if b < 3:
                            p.tt("pool", acc[ct][:], acc[ct][:], sg[:], ALU.add, [acc[ct], sg], [acc[ct]])
                        else:
                            ob = outb[ct % 2]
                            p.tt("pool", ob[:], acc[ct][:], sg[:], ALU.add, [acc[ct], sg], [ob])
                            r0 = cc * 512 + ct * 128
                            p.dma("sp", g.MT[r0:r0 + 128, s * 512:(s + 1) * 512], ob[:], [ob], [g.MT])
    p.end_phase()


def resid_gemm_phase(g, srcT, KTn, W, gcols, xin, xout):
    p = g.p
    p.begin_phase()
    ws = WStream(p)
    ATs = [p.sb([128, KTn, 512], BF16, 'AT') for _ in range(2 if KTn <= 32 else 1)]
    gbcs = []
    for gcol in gcols:
        gb1 = p.sb([128, D], F32, 'gbc')
        gate_bc(g, gb1, gcol)
        gbcs.append(gb1)
    xr = [p.sb([128, 512], F32, 'xr') for _ in range(4)]
    n = 0
    bi = 0
    for s in range(g.NS):
        AT = ATs[s % len(ATs)]
        load_AT(g, AT, srcT, KTn, s)
        for cc in range(8):
            banks = g.bank[bi * 4:(bi + 1) * 4]
            bi ^= 1
            gemm_chunk(g, ws, AT, KTn, W, cc * 512, 512, 'tm', banks)
            for tb in range(4):
                r0 = (s * 4 + tb) * 128
                gbc = gbcs[(s * 4 + tb) // g.SB]
                x = xr[n % 4]
                n += 1
                p.dma("sp", x[:], xin[r0:r0 + 128, cc * 512:(cc + 1) * 512], [xin], [x])
                t = p.tt("dve", banks[tb][:, :], banks[tb][:, :], gbc[:, cc * 512:(cc + 1) * 512], ALU.mult,
                         [banks[tb], gbc], [banks[tb]])
                p.tt("dve", x[:], x[:], banks[tb][:, :], ALU.add, [x, banks[tb]], [x])
                p.dma("sp", xout[r0:r0 + 128, cc * 512:(cc + 1) * 512], x[:], [x], [xout])
    p.end_phase()


def ffn1_phase(g, l):
    p = g.p
    p.begin_phase()
    W1 = g.I['ffn_w1'].ap()[l]
    W3 = g.I['ffn_w3'].ap()[l]
    ws = WStream(p)
    ATs = [p.sb([128, 32, 512], BF16, 'AT') for _ in range(2)]
    sl = [p.sb([128, 512], F32, 'sl') for _ in range(2)]
    ub = [p.sb([128, 512], BF16, 'ub') for _ in range(2)]
    n = 0
    for s in range(g.NS):
        AT = ATs[s % 2]
        load_AT(g, AT, g.HT, 32, s)
        for cc in range(22):
            w = 512 if cc < 21 else 256
            a = g.bank[0:4]
            b = g.bank[4:8]
            gemm_chunk(g, ws, AT, 32, W1, cc * 512, w, 'fm', a)
            gemm_chunk(g, ws, AT, 32, W3, cc * 512, w, 'fm', b)
            for ct in range(w // 128):
                sg = sl[n % 2]
                u = ub[n % 2]
                n += 1
                p.act(sg[:], a[ct][:, :], AF.Silu, [a[ct]], [sg])
                p.tt("dve", u[:], sg[:], b[ct][:, :], ALU.mult, [sg, b[ct]], [u])
                r0 = cc * 512 + ct * 128
                p.dma("sp", g.UT[r0:r0 + 128, s * 512:(s + 1) * 512], u[:], [u], [g.UT])
    p.end_phase()


def attn_prepass(g, l, qc0, kc0, nk, gqn, gkn, rope):
    p, I = g.p, g.I
    nh = 8 + nk
    gq = p.sb([128, 128], F32, 'gq')
    gk = p.sb([128, 128], F32, 'gk')
    p.dma("sp", gq[:], bass.AP(I[gqn], l * 128, [[0, 128], [1, 128]]), [], [gq])
    p.dma("sp", gk[:], bass.AP(I[gkn], l * 128, [[0, 128], [1, 128]]), [], [gk])
    xs = [p.sb([128, nh, 128], F32, 'qk') for _ in range(2)]
    t1 = p.sb([128, nh, 128], F32, 'qkt')
    t2 = p.sb([128, nh, 64], F32, 'qkt2')
    t3 = p.sb([128, nh, 64], F32, 'qkt3')
    rp = [p.sb([128, 128], F32, 'rp') for _ in range(2)]
    st = [p.sb([128, 3, nh], F32, 'qst') for _ in range(2)]
    ob = [p.sb([128, nh, 128], BF16, 'qkb') for _ in range(2)]
    for tb in range(g.NB):
        x = xs[tb % 2]
        s = st[tb % 2]
        o = ob[tb % 2]
        r0 = tb * 128
        p.dma("sp", x[:, 0:8, :], g.PTM[r0:r0 + 128, qc0:qc0 + 1024].rearrange("p (h d) -> p h d", d=128), [g.PTM], [x])
        p.dma("sp", x[:, 8:nh, :], g.PTM[r0:r0 + 128, kc0:kc0 + nk * 128].rearrange("p (h d) -> p h d", d=128), [g.PTM], [x])
        p.tt("dve", t1[:], x[:], x[:], ALU.mult, [x], [t1])
        p.op("dve", lambda e: e.tensor_reduce(out=s[:, 0, :], in_=t1[:], axis=AX.X, op=ALU.add), [t1], [s])
        p.ts("dve", s[:, 1, :], s[:, 0, :], 1.0 / 128, EPS, ALU.mult, ALU.add, [s], [s])
        p.act(s[:, 2, :], s[:, 1, :], AF.Sqrt, [s], [s])
        p.op("dve", lambda e: e.reciprocal(out=s[:, 0, :], in_=s[:, 2, :]), [s], [s])
        p.tt("dve", x[:], x[:], s[:, 0, :].unsqueeze(2).to_broadcast([128, nh, 128]), ALU.mult, [x, s], [x])
        p.tt("pool", x[:, 0:8, :], x[:, 0:8, :], gq[:].unsqueeze(1).to_broadcast([128, 8, 128]), ALU.mult, [x, gq], [x])
        p.tt("pool", x[:, 8:nh, :], x[:, 8:nh, :], gk[:].unsqueeze(1).to_broadcast([128, nk, 128]), ALU.mult, [x, gk], [x])
        if rope:
            r = rp[tb % 2]
            p.dma("sp", r[:], I['rope'].ap()[r0:r0 + 128, :], [], [r])
            cs = r[:, 0:64].unsqueeze(1).to_broadcast([128, nh, 64])
            sn = r[:, 64:128].unsqueeze(1).to_broadcast([128, nh, 64])
            x1 = x[:, :, 0:64]
            x2 = x[:, :, 64:128]
            p.tt("dve", t2[:], x2, sn, ALU.mult, [x, r], [t2])
            p.tt("dve", t3[:], x1, sn, ALU.mult, [x, r], [t3])
            p.tt("dve", t1[:, :, 0:64], x1, cs, ALU.mult, [x, r], [t1])
            p.tt("dve", t1[:, :, 64:128], x2, cs, ALU.mult, [x, r], [t1])
            p.tt("dve", x[:, :, 0:64], t1[:, :, 0:64], t2[:], ALU.subtract, [t1, t2], [x])
            p.tt("dve", x[:, :, 64:128], t1[:, :, 64:128], t3[:], ALU.add, [t1, t3], [x])
        for hh in range(nh):
            bk = g.bank[(hh // 4) % 4]
            p.tr(bk[:, (hh % 4) * 128:(hh % 4 + 1) * 128], x[:, hh, :], g.ident, [x, g.CT], [bk])
            if hh % 4 == 3 or hh == nh - 1:
                h0 = (hh // 4) * 4
                n = hh - h0 + 1
                evac(p, hh // 4, o[:, h0:h0 + n, :], bk[:, 0:n * 128].rearrange("p (h t) -> p h t", t=128), [bk], [o])
        p.dma("sp", g.QT[:, r0:r0 + 128].rearrange("(h p) t -> p h t", p=128), o[:, 0:8, :], [o], [g.QT])
        p.dma("sp", g.KTd[0:nk * 128, r0:r0 + 128].rearrange("(h p) t -> p h t", p=128), o[:, 8:nh, :], [o], [g.KTd])


def attn_main(g, l, b, nk, vc0, jlist, maskfn, esink):
    p = g.p
    T, NB = g.T, g.NB
    grp = 8 // nk
    qT = [p.sb([128, T], BF16, 'qT') for _ in range(2)]
    kT = [p.sb([128, T], BF16, 'kT') for _ in range(2)]
    vf = p.sb([128, NB, 128], F32, 'vf')
    va = [p.sb([128, NB, 129], BF16, 'va') for _ in range(2)]
    oT = [p.sb([128, T], BF16, 'oT') for _ in range(2)]
    PT = [p.sb([128, 8, 128], F32, 'PT') for _ in range(2)]
    PTb = [p.sb([128, 8, 128], BF16, 'PTb') for _ in range(2)]
    otm = [p.sb([128, 128], F32, 'otm') for _ in range(2)]
    dn = [p.sb([128, 2], F32, 'dn') for _ in range(2)]
    scale = 128.0 ** -0.5
    n = 0
    for h in range(8):
        kvh = h // grp
        q = qT[h % 2]
        k = kT[h % 2]
        v = va[h % 2]
        o = oT[h % 2]
        p.dma("sp", q[:], g.QT[h * 128:(h + 1) * 128, :], [g.QT], [q])
        p.dma("sp", k[:], g.KTd[kvh * 128:(kvh + 1) * 128, :], [g.KTd], [k])
        p.dma("sp", vf[:], g.PTM[:, vc0 + kvh * 128:vc0 + (kvh + 1) * 128].rearrange("(b p) d -> p b d", p=128), [g.PTM], [vf])
        p.cp("pool", v[:, :, 0:128], vf[:], [vf], [v])
        p.op("pool", lambda e: e.memset(v[:, :, 128:129], 1.0), [], [v])
        for i in range(NB):
            js = jlist(i)
            nj = len(js)
            pt = PT[n % 2]
            ptb = PTb[n % 2]
            sb = [g.bank[(n % 2) * 2], g.bank[(n % 2) * 2 + 1]]
            obk = g.bank[4 + n % 2]
            tbk = g.bank[6 + n % 2]
            om = otm[n % 2]
            d = dn[n % 2]
            n += 1
            for idx, j in enumerate(js):
                bk = sb[idx // 4]
                p.mm(bk[:, (idx % 4) * 128:(idx % 4 + 1) * 128], k[:, j * 128:(j + 1) * 128], q[:, i * 128:(i + 1) * 128],
                     True, True, [k, q], [bk])
            for half in range((nj + 3) // 4):
                cnt = min(4, nj - half * 4)
                p.act(pt[:, half * 4:half * 4 + cnt, :], sb[half][:, 0:cnt * 128].rearrange("p (j t) -> p j t", t=128),
                      AF.Exp, [sb[half]], [pt], scale=scale)
            for idx, j in enumerate(js):
                if not maskfn(i, j, h, pt, ptb, idx):
                    p.cp("pool", ptb[:, idx, :], pt[:, idx, :], [pt], [ptb])
            for idx, j in enumerate(js):
                p.mm(obk[:, 0:129], ptb[:, idx, :], v[:, j, :], idx == 0, idx == nj - 1, [ptb, v], [obk])
            if esink is not None:
                p.tt("dve", d[:, 0:1], obk[:, 128:129], esink[:, h:h + 1], ALU.add, [obk, esink], [d])
                p.op("dve", lambda e: e.reciprocal(out=d[:, 1:2], in_=d[:, 0:1]), [d], [d])
            else:
                p.op("dve", lambda e: e.reciprocal(out=d[:, 1:2], in_=obk[:, 128:129]), [obk], [d])
            p.act(om[:], obk[:, 0:128], AF.Copy, [obk, d], [om], scale=d[:, 1:2])
            p.tr(tbk[:, 0:128], om[:], g.ident, [om, g.CT], [tbk])
            p.cp("dve", o[:, i * 128:(i + 1) * 128], tbk[:, 0:128], [tbk], [o])
        p.dma("sp", g.OT[b][h * 128:(h + 1) * 128, :], o[:], [o], [g.OT[b]])


def swa_phase(g, l):
    p, I = g.p, g.I
    p.begin_phase()
    attn_prepass(g, l, 1024, 2048, 2, 'swa_q_norm_g', 'swa_k_norm_g', True)
    p.end_phase()
    p.begin_phase()
    es = p.sb([128, 8], F32, 'esink')
    p.dma("sp", es[:], bass.AP(I['swa_sink'], l * 8, [[0, 128], [1, 8]]), [], [es])
    p.act(es[:], es[:], AF.Exp, [es], [es])
    NB = g.NB

    def jlist(i):
        return [j for j in (i - 1, i, i + 1) if 0 <= j < NB]

    def maskfn(i, j, h, pt, ptb, idx):
        if j == i:
            return False
        m = g.trige if j < i else g.trile
        if (i // g.SB) != (j // g.SB):
            p.stt(ptb[:, idx, :], pt[:, idx, :], g.LK[:, 0:1], m, ALU.mult, ALU.mult, [pt, g.CT, g.LK], [ptb])
        else:
            p.tt("dve", ptb[:, idx, :], pt[:, idx, :], m, ALU.mult, [pt, g.CT], [ptb])
        return True

    attn_main(g, l, 1, 2, 2304, jlist, maskfn, es)
    p.end_phase()


def na_phase(g, l):
    p, I = g.p, g.I
    p.begin_phase()
    attn_prepass(g, l, 2560, 3584, 8, 'na_q_norm_g', 'na_k_norm_g', False)
    p.end_phase()
    p.begin_phase()
    R = g.T // 64
    z = p.sb([8, 465], F32, 'rpbt')
    zz = p.sb([1, 128], F32, 'zz')
    p.op("dve", lambda e: e.memset(zz[:], 0.0), [], [zz])
    p.dma("sp", z[:], I['na_rpb'].ap()[l].rearrange("h a b -> h (a b)"), [], [z])
    p.dma("sp", g.RPB[0:128].rearrange("(o n) -> o n", o=1), zz[:], [zz], [g.RPB])
    p.dma("sp", g.RPB[128 + 3720:128 + 3720 + 128].rearrange("(o n) -> o n", o=1), zz[:], [zz], [g.RPB])
    p.dma("sp", g.RPB[128:128 + 3720].rearrange("(h n) -> h n", h=8), z[:], [z], [g.RPB])
    EB = p.sb([128, 8, 15, 64], F32, 'EB')
    hk = [p.sb([64, 15, 64], F32, 'hk') for _ in range(2)]
    hd = [p.sb([64, 15, 2, 64], F32, 'hd') for _ in range(2)]
    cv = g.CT[:, 576:640]
    J = g.CT[0:64, 512:576]
    for h in range(8):
        a = hk[h % 2]
        d2 = hd[h % 2]
        p.dma("sp", a[:], bass.AP(g.RPB.t.tensor, 128 + h * 465 - 48, [[1, 64], [31, 15], [1, 64]]), [g.RPB], [a])
        p.act(d2[:, :, 0, :], a[:], AF.Exp, [a], [d2])
        p.cp("dve", d2[:, :, 1, :], d2[:, :, 0, :], [d2], [d2])
        for half in range(2):
            bk = g.bank[(h * 2 + half) % 4]
            cnt = 8 if half == 0 else 7
            for dd in range(cnt):
                dri = half * 8 + dd
                p.mm(bk[:, dd * 64:(dd + 1) * 64], d2[:, dri, :, :].rearrange("p a b -> p (a b)"), J, True, True, [d2, g.CT], [bk])
            p.tt("dve", EB[:, h, half * 8:half * 8 + cnt, :], bk[:, 0:cnt * 64].rearrange("p (a b) -> p a b", b=64),
                 cv.unsqueeze(1).to_broadcast([128, cnt, 64]), ALU.mult, [bk, g.CT], [EB])

    Rs = R // g.NSEG

    def quad(i, j):
        res = []
        for kq in range(2):
            for qq in range(2):
                qr, kr = 2 * i + qq, 2 * j + kq
                rl = min(max(qr - 4, 0), R - 8)
                okL = rl <= kr <= rl + 7
                base = (qr // Rs) * Rs
                ru = base + min(max(qr - base - 4, 0), Rs - 8)
                okU = ru <= kr <= ru + 7
                if g.NSEG == 1:
                    code = 1 if okL else 0
                else:
                    code = 1 if (okL and okU) else (2 if okL else (3 if okU else 0))
                res.append((kq, qq, code, kr - qr + 7))
        return res

    def jlist(i):
        return [j for j in range(max(0, i - 4), min(g.NB, i + 5)) if any(q[2] for q in quad(i, j))]

    def maskfn(i, j, h, pt, ptb, idx):
        for (kq, qq, code, dri) in quad(i, j):
            ps_ = slice(kq * 64, (kq + 1) * 64)
            fs = slice(qq * 64, (qq + 1) * 64)
            if code == 1:
                p.tt("dve", ptb[ps_, idx, fs], pt[ps_, idx, fs], EB[ps_, h, dri, :], ALU.mult, [pt, EB], [ptb])
            elif code == 0:
                p.op("pool", lambda e: e.memset(ptb[ps_, idx, fs], 0.0), [], [ptb])
            else:
                lc = g.LK[ps_, 0:1] if code == 2 else g.LK[ps_, 1:2]
                p.stt(ptb[ps_, idx, fs], pt[ps_, idx, fs], lc, EB[ps_, h, dri, :], ALU.mult, ALU.mult, [pt, EB, g.LK], [ptb])
        return True

    attn_main(g, l, 3, 8, 4608, jlist, maskfn, None)
    p.end_phase()


def bc_rows(g, name, l, n, width=None):
    p = g.p
    t = p.sb([128, n], F32, 'bc')
    p.dma("sp", t[:], bass.AP(g.I[name], l * n, [[0, 128], [1, n]]), [], [t])
    return t


def ssd_phase(g, l):
    p, I = g.p, g.I
    T, NB = g.T, g.NB
    p.begin_phase()
    bk0 = g.bank[7]
    cwt = p.sb([6, 1536], F32, 'cwt')
    p.dma("sp", cwt[0:5, :], I['ssd_conv_w'].ap()[l], [], [cwt])
    p.dma("sp", cwt[5:6, :], I['ssd_conv_b'].ap()[l].rearrange("(o n) -> o n", o=1), [], [cwt])
    cw = p.sb([128, 12, 8], F32, 'cw')
    for c in range(12):
        p.tr(bk0[:, c * 8:c * 8 + 6], cwt[0:6, c * 128:(c + 1) * 128], g.ident[0:6, 0:6], [cwt, g.CT], [bk0])
    p.cp("dve", cw[:], bk0[:, 0:96].rearrange("p (c j) -> p c j", j=8), [bk0], [cw])
    TM = p.sb([128, NB, 2, 3, 16], F32, 'TM')
    p.push()
    onesT = p.sb([16, T], F32, 'onesT')
    p.op("dve", lambda e: e.memset(onesT[:], 1.0), [], [onesT])
    for d in range(2):
        raw = p.sb([16, T], F32, 'raw')
        w1 = p.sb([16, T], F32, 'w1')
        w2 = p.sb([16, T], F32, 'w2')
        R = p.sb([16, T], F32, 'R')
        col = p.sb([16, 2], F32, 'col')
        p.dma("sp", raw[:], g.PFM[1536 + 16 * d:1552 + 16 * d, :], [g.PFM], [raw])
        p.dma("sp", col[:, 0:1], bass.AP(I['ssd_dt_bias'], l * 32 + d * 16, [[1, 16], [1, 1]]), [], [col])
        p.dma("sp", col[:, 1:2], bass.AP(I['ssd_a_log'], l * 32 + d * 16, [[1, 16], [1, 1]]), [], [col])
        p.act(col[:, 1:2], col[:, 1:2], AF.Exp, [col], [col])
        p.ts("dve", col[:, 1:2], col[:, 1:2], -1.0, None, ALU.mult, None, [col], [col])
        p.ts("dve", raw[:], raw[:], col[:, 0:1], None, ALU.add, None, [raw, col], [raw])
        p.act(w1[:], raw[:], AF.Abs, [raw], [w1])
        p.act(w1[:], w1[:], AF.Exp, [w1], [w1], scale=-1.0)
        p.act(w1[:], w1[:], AF.Ln, [w1], [w1], bias=1.0)
        p.stt(w1[:], raw[:], 0.0, w1[:], ALU.max, ALU.add, [raw, w1], [w1])
        p.act(w2[:], w1[:], AF.Ln, [w1], [w2])
        p.ts("dve", w1[:], w1[:], col[:, 1:2], None, ALU.mult, None, [w1, col], [w1])
        p.op("dve", lambda e: e.tensor_tensor_scan(out=R[:], data0=onesT[:], data1=w1[:], initial=0.0,
                                                   op0=ALU.mult, op1=ALU.add), [onesT, w1], [R])
        if d == 1:
            p.tt("dve", R[:], R[:], w1[:], ALU.subtract, [R, w1], [R])
            p.tt("dve", w1[:], w2[:], R[:], ALU.add, [w2, R], [w1])
        else:
            p.tt("dve", w1[:], w2[:], R[:], ALU.subtract, [w2, R], [w1])
        for tb in range(NB):
            bk = g.bank[4 + tb % 2]
            for qi, src in enumerate((R, w2, w1)):
                p.tr(bk[:, qi * 16:(qi + 1) * 16], src[:, tb * 128:(tb + 1) * 128], g.ident[0:16, 0:16], [src, g.CT], [bk])
            p.cp("act" if tb % 2 else "dve", TM[:, tb, d, :, :], bk[:, 0:48].rearrange("p (q h) -> p q h", h=16), [bk], [TM])
        p.dma("sp", g.RR[16 * d:16 * d + 16, :], R[:], [R], [g.RR])
    p.pop()
    BT = p.sb([128, 2, T], BF16, 'BT')
    CTt = p.sb([128, 2, T], BF16, 'CTt')
    xsb = p.sb([128, NB, 1024], BF16, 'xsb')
    p.push()
    NSEG = g.NSEG
    H = T // NSEG
    XW = T + 4 * NSEG
    xps = [p.sb([128, XW], F32, 'xp') for _ in range(1)]
    accs = [p.sb([128, T], F32, 'cacc') for _ in range(1)]
    stg = [p.sb([128, 4, 128], F32, 'cstg') for _ in range(2)]
    for xp in xps:
        p.op("dve", lambda e: e.memset(xp[:, 0:2], 0.0), [], [xp])
        p.op("dve", lambda e: e.memset(xp[:, XW - 2:XW], 0.0), [], [xp])
    n = 0
    for c in range(12):
        xp = xps[0]
        acc = accs[0]
        for sg in range(NSEG):
            b0 = 2 + sg * (H + 4)
            p.dma("sp", xp[:, b0:b0 + H], g.PFM[c * 128:(c + 1) * 128, sg * H:(sg + 1) * H], [g.PFM], [xp])
        if NSEG == 2:
            p.ts("dve", xp[:, 2 + H:4 + H], xp[:, 6 + H:8 + H], g.LK[:, 0:1], None, ALU.mult, None, [xp, g.LK], [xp])
            p.ts("dve", xp[:, 4 + H:6 + H], xp[:, H:2 + H], g.LK[:, 0:1], None, ALU.mult, None, [xp, g.LK], [xp])
        for sg in range(NSEG):
            w0 = sg * (H + 4)
            a = acc[:, sg * H:(sg + 1) * H]
            p.ts("dve", a, xp[:, w0:w0 + H], cw[:, c, 0:1], None, ALU.mult, None, [xp, cw], [acc])
            for jj in range(1, 5):
                p.stt(a, xp[:, w0 + jj:w0 + jj + H], cw[:, c, jj:jj + 1], a, ALU.mult, ALU.add, [xp, cw, acc], [acc])
        if c >= 8:
            dst = BT if c < 10 else CTt
            p.act(dst[:, c % 2, :], acc[:], AF.Silu, [acc, cw], [dst], bias=cw[:, c, 5:6])
        else:
            p.act(acc[:], acc[:], AF.Silu, [acc, cw], [acc], bias=cw[:, c, 5:6])
            for q4 in range(NB // 4):
                bk = g.bank[n % 4]
                sg = stg[n % 2]
                n += 1
                for t4 in range(4):
                    tb = q4 * 4 + t4
                    p.tr(bk[:, t4 * 128:(t4 + 1) * 128], acc[:, tb * 128:(tb + 1) * 128], g.ident, [acc, g.CT], [bk])
                p.cp("act", sg[:], bk[:, :].rearrange("p (b c) -> p b c", c=128), [bk], [sg])
                p.cp("pool", xsb[:, q4 * 4:q4 * 4 + 4, c * 128:(c + 1) * 128], sg[:], [sg], [xsb])
                p.dma("sp", g.XS[q4 * 512:(q4 + 1) * 512, c * 128:(c + 1) * 128].rearrange("(b p) c -> p b c", p=128),
                      sg[:], [sg], [g.XS])
    p.pop()
    p.push()
    SELS = p.sb([16, 16, 128], F32, 'SELS')
    for h in range(16):
        p.cp("pool", SELS[:, h, :], g.ident[0:16, h:h + 1].to_broadcast([16, 128]), [g.CT], [SELS])
    d16 = bc_rows(g, 'ssd_d', l, 16)
    Dbc = p.sb([128, 16, 64], F32, 'Dbc')
    p.cp("dve", Dbc[:], d16[:].unsqueeze(2).to_broadcast([128, 16, 64]), [d16], [Dbc])
    NG = bc_rows(g, 'ssd_norm_g', l, 1024)
    RBi = [p.sb([128, 32, 128], F32, 'RBi') for _ in range(1)]
    Ri = [p.sb([16, 2, 128], F32, 'Ri') for _ in range(2)]
    Gm = [p.sb([128, 2, 2, 128], F32, 'Gm') for _ in range(2)]
    Pt = [p.sb([128, 128], F32, 'Pt') for _ in range(4)]
    Mt = [p.sb([128, 128], BF16, 'Mt') for _ in range(4)]
    zt = [p.sb([128, 1024], F32, 'zt') for _ in range(1)]
    xsi = [p.sb([128, 1024], F32, 'xsi') for _ in range(1)]
    yt = [p.sb([128, 1024], F32, 'yt') for _ in range(1)]
    junk = p.sb([128, 1024], BF16, 'junk')
    sst = [p.sb([128, 4], F32, 'sst') for _ in range(2)]
    osb = [p.sb([128, 8, 128], BF16, 'osb') for _ in range(2)]
    n = 0
    gn = 0
    for i in range(NB):
        rb = RBi[0]
        z = zt[0]
        xs = xsi[0]
        y = yt[0]
        ri = Ri[i % 2]
        p.dma("sp", ri[:], g.RR[:, i * 128:(i + 1) * 128].rearrange("(d h) t -> h d t", d=2), [g.RR], [ri])
        s = sst[i % 2]
        ob = osb[i % 2]
        yb = [g.bank[0], g.bank[1]]
        p.dma("sp", z[:], g.PTM[i * 128:(i + 1) * 128, 0:1024], [g.PTM], [z])
        p.dma("sp", xs[:], g.XS[i * 128:(i + 1) * 128, :], [g.XS], [xs])
        for hd4 in range(8):
            bk = g.bank[4 + hd4 % 4]
            for q in range(4):
                hd = hd4 * 4 + q
                p.mm(bk[:, q * 128:(q + 1) * 128], SELS[:, hd % 16, :], ri[:, hd // 16, :], True, True,
                     [SELS, ri], [bk])
            evac(p, hd4, rb[:, hd4 * 4:hd4 * 4 + 4, :], bk[:, :].rearrange("p (q t) -> p q t", t=128), [bk], [rb])
        p.op("dve", lambda e: e.memset(yb[0][:, :], 0.0), [], [yb[0]])
        p.op("dve", lambda e: e.memset(yb[1][:, :], 0.0), [], [yb[1]])
        for j in range(NB):
            gb = g.bank[2 + gn % 2]
            gn += 1
            for gg in range(2):
                p.mm(gb[:, gg * 128:(gg + 1) * 128], BT[:, gg, j * 128:(j + 1) * 128], CTt[:, gg, i * 128:(i + 1) * 128],
                     True, True, [BT, CTt], [gb])
            dirs = [0] if j < i else ([1] if j > i else [0, 1])
            gm = None
            if j == i:
                gm = Gm[i % 2]
                for d in range(2):
                    m = g.trile if d == 0 else g.trige
                    p.tt("dve", gm[:, d, :, :], gb[:, 0:256].rearrange("p (a b) -> p a b", b=128),
                         m.unsqueeze(1).to_broadcast([128, 2, 128]), ALU.mult, [gb, g.CT], [gm])
            for d in dirs:
                sc = 1.0 if d == 0 else -1.0
                for h in range(16):
                    hd = d * 16 + h
                    pt = Pt[n % 4]
                    mt = Mt[n % 4]
                    n += 1
                    if j == i:
                        p.ts("dve", pt[:], rb[:, hd, :], TM[:, j, d, 0, h:h + 1], 0.0, ALU.subtract,
                             ALU.min if d == 0 else ALU.max, [rb, TM], [pt])
                        p.act(pt[:], pt[:], AF.Exp, [pt, TM], [pt], scale=sc, bias=TM[:, j, d, 1, h:h + 1])
                        p.tt("dve", mt[:], pt[:], gm[:, d, h // 8, :], ALU.mult, [pt, gm], [mt])
                    else:
                        p.act(pt[:], rb[:, hd, :], AF.Exp, [rb, TM], [pt], scale=sc, bias=TM[:, j, d, 2, h:h + 1])
                        if (i // g.SB) != (j // g.SB):
                            p.stt(mt[:], pt[:], g.LK[:, 0:1], gb[:, (h // 8) * 128:(h // 8 + 1) * 128], ALU.mult, ALU.mult,
                                  [pt, gb, g.LK], [mt])
                        else:
                            p.tt("dve", mt[:], pt[:], gb[:, (h // 8) * 128:(h // 8 + 1) * 128], ALU.mult, [pt, gb], [mt])
                    p.mm(yb[h // 8][:, (h % 8) * 64:(h % 8 + 1) * 64], mt[:], xsb[:, j, h * 64:(h + 1) * 64], False, False,
                         [mt, xsb], [yb[h // 8]], ms=True, sgc=True)
        p.tt("dve", y[:], xs[:], Dbc[:].rearrange("p a b -> p (a b)"), ALU.mult, [xs, Dbc], [y])
        for hf in range(2):
            p.tt("dve", y[:, hf * 512:(hf + 1) * 512], y[:, hf * 512:(hf + 1) * 512], yb[hf][:, :], ALU.add, [y, yb[hf]], [y])
        p.act(z[:], z[:], AF.Silu, [z], [z])
        p.tt("dve", y[:], y[:], z[:], ALU.mult, [y, z], [y])
        p.stt(junk[:], y[:], 1.0, y[:], ALU.mult, ALU.mult, [y], [junk, s], accum_out=s[:, 0:1])
        p.ts("dve", s[:, 1:2], s[:, 0:1], 1.0 / 1024, EPS, ALU.mult, ALU.add, [s], [s])
        p.act(s[:, 2:3], s[:, 1:2], AF.Sqrt, [s], [s])
        p.op("dve", lambda e: e.reciprocal(out=s[:, 3:4], in_=s[:, 2:3]), [s], [s])
        p.stt(y[:], y[:], s[:, 3:4], NG[:], ALU.mult, ALU.mult, [y, s, NG], [y])
        for k in range(8):
            bk = g.bank[4 + (k // 4)]
            p.tr(bk[:, (k % 4) * 128:(k % 4 + 1) * 128], y[:, k * 128:(k + 1) * 128], g.ident, [y, g.CT], [bk])
            if k % 4 == 3:
                evac(p, k // 4, ob[:, k - 3:k + 1, :], bk[:, :].rearrange("p (a b) -> p a b", b=128), [bk], [ob])
        p.dma("sp", g.OT[0][:, i * 128:(i + 1) * 128].rearrange("(k p) t -> p k t", p=128), ob[:], [ob], [g.OT[0]])
    p.pop()
    p.end_phase()


def range_reduce(p, r, a, tmp):
    rb, rap = r
    ab, aap = a
    tb, tap = tmp
    p.ts("dve", tap, aap, 1.0 / (2 * PI), MAGIC, ALU.mult, ALU.add, [ab], [tb])
    p.ts("dve", tap, tap, MAGIC, None, ALU.subtract, None, [tb], [tb])
    p.stt(rap, tap, -2 * PI, aap, ALU.mult, ALU.add, [tb, ab], [rb])
    p.ts("dve", rap, rap, PI, -PI, ALU.min, ALU.max, [rb], [rb])


def sincos(p, sn, cs, ang, tmp):
    range_reduce(p, ang, ang, tmp)
    p.act(sn[1], ang[1], AF.Sin, [ang[0]], [sn[0]])
    p.act(tmp[1], ang[1], AF.Abs, [ang[0]], [tmp[0]])
    p.act(cs[1], tmp[1], AF.Sin, [tmp[0]], [cs[0]], scale=-1.0, bias=PI / 2)


def s5_phase(g, l):
    p, I = g.p, g.I
    T, NB = g.T, g.NB
    p.begin_phase()
    bk7 = g.bank[7]
    tmp = p.sb([128, 128], F32, 'tmp')
    Bblk = p.sb([128, 8, 2, 512], BF16, 'Bblk')
    S = [[p.sb([128, 128], F32, 'S') for _ in range(2)] for _ in range(4)]
    for kk in range(4):
        for ri in range(2):
            p.op("dve", lambda e: e.memset(S[kk][ri][:], 0.0), [], [S[kk][ri]])
    n = 0
    for ct in range(8):
        for kk in range(4):
            for ri, nm in enumerate(('s5_b_re', 's5_b_im')):
                st = S[kk][ri]
                for gs in range(2):
                    gg = ct * 8 + 2 * kk + gs
                    p.dma("sp", st[gs * 64:(gs + 1) * 64, (2 * kk + gs) * 16:(2 * kk + gs + 1) * 16], I[nm].ap()[l, gg], [], [st])
                bk = g.bank[4 + n % 2]
                n += 1
                p.tr(bk[:, 0:128], st[:], g.ident, [st, g.CT], [bk])
                evac(p, n, Bblk[:, ct, ri, kk * 128:(kk + 1) * 128], bk[:, 0:128], [bk], [Bblk])
    dsk = p.sb([128, 8], F32, 'dsk')
    load_T(g, None, I['s5_d'].ap()[l].rearrange("(k p) -> k p", p=128), 8, tmp, bk7)
    p.cp("dve", dsk[:], bk7[:, 0:8], [bk7], [dsk])
    glb = p.sb([128, 8], F32, 'glb')
    load_T(g, None, I['s5_glu_b'].ap()[l].rearrange("(k p) -> k p", p=128), 8, tmp, bk7)
    p.cp("dve", glb[:], bk7[:, 0:8], [bk7], [glb])
    Cblk = p.sb([128, 32, 2, 128], BF16, 'Cblk')
    S2 = [[p.sb([128, 128], F32, 'S2') for _ in range(2)] for _ in range(4)]
    for kk in range(4):
        for ri in range(2):
            p.op("dve", lambda e: e.memset(S2[kk][ri][:], 0.0), [], [S2[kk][ri]])
    AinT = p.sb([128, 2, 4096], F32, 'AinT')
    Aout = p.sb([128, 2, 32, 128], F32, 'Aout')
    for d in range(2):
        for k in range(32):
            ct, kk = k // 4, k % 4
            for ri, nm in enumerate(('s5_c_re', 's5_c_im')):
                st = S2[kk][ri]
                for gs in range(2):
                    gl = 2 * kk + gs
                    p.dma("sp", st[gl * 16:(gl + 1) * 16, gs * 64:(gs + 1) * 64], I[nm].ap()[l, d, ct * 8 + gl], [], [st])
                bk = g.bank[4 + n % 2]
                n += 1
                p.tr(bk[:, 0:128], st[:], g.ident, [st, g.CT], [bk])
                if ri == 0:
                    p.cp("dve", Cblk[:, k, 0, :], bk[:, 0:128], [bk], [Cblk])
                else:
                    p.ts("dve", Cblk[:, k, 1, :], bk[:, 0:128], -1.0, None, ALU.mult, None, [bk], [Cblk])
        p.push()
        mcol = g.CT[:, 640 + d:641 + d]
        negm = g.CT[:, 642 + d:643 + d]
        mrow = g.CT[:, 644 + 128 * d:772 + 128 * d]
        W = 1024
        tl = [p.sb([128, W], F32, 'tb%d' % i) for i in range(10)]
        ls = p.sb([128, 64], F32, 'ls')
        p.dma("sp", ls[:], bass.AP(I['s5_log_step'], (l * 2 + d) * 64, [[0, 128], [1, 64]]), [], [ls])
        p.act(ls[:], ls[:], AF.Exp, [ls], [ls])
        for q in range(4096 // W):
            ar, ai, al, be, t1, t2, t3, t4, t5, t6 = tl
            o0 = (l * 2 + d) * 4096 + q * W
            p.dma("sp", ar[:], bass.AP(I['s5_a_re'], o0, [[0, 128], [1, W]]), [], [ar])
            p.dma("sp", ai[:], bass.AP(I['s5_a_im'], o0, [[0, 128], [1, W]]), [], [ai])
            stb = ls[:, q * (W // 64):(q + 1) * (W // 64)].unsqueeze(2).to_broadcast([128, W // 64, 64])
            v3 = lambda t: t[:].rearrange("p (a b) -> p a b", b=64)
            p.tt("dve", v3(al), v3(ar), stb, ALU.mult, [ar, ls], [al])
            p.tt("dve", v3(be), v3(ai), stb, ALU.mult, [ai, ls], [be])
            p.act(t1[:], al[:], AF.Exp, [al], [t1])
            p.cp("dve", t6[:], be[:], [be], [t6])
            sincos(p, (t2, t2[:]), (t3, t3[:]), (t6, t6[:]), (t4, t4[:]))
            p.tt("dve", t3[:], t3[:], t1[:], ALU.mult, [t3, t1], [t3])
            p.tt("dve", t2[:], t2[:], t1[:], ALU.mult, [t2, t1], [t2])
            p.ts("dve", t3[:], t3[:], -1.0, None, ALU.add, None, [t3], [t3])
            p.tt("dve", t4[:], ar[:], ar[:], ALU.mult, [ar], [t4])
            p.tt("dve", t1[:], ai[:], ai[:], ALU.mult, [ai], [t1])
            p.tt("dve", t4[:], t4[:], t1[:], ALU.add, [t4, t1], [t4])
            p.op("dve", lambda e: e.reciprocal(out=t4[:], in_=t4[:]), [t4], [t4])
            p.tt("dve", t1[:], t3[:], ar[:], ALU.mult, [t3, ar], [t1])
            p.tt("dve", t5[:], t2[:], ai[:], ALU.mult, [t2, ai], [t5])
            p.tt("dve", t1[:], t1[:], t5[:], ALU.add, [t1, t5], [t1])
            p.tt("dve", t1[:], t1[:], t4[:], ALU.mult, [t1, t4], [t1])
            p.tt("dve", t5[:], t2[:], ar[:], ALU.mult, [t2, ar], [t5])
            p.tt("dve", t6[:], t3[:], ai[:], ALU.mult, [t3, ai], [t6])
            p.tt("dve", t5[:], t5[:], t6[:], ALU.subtract, [t5, t6], [t5])
            p.tt("dve", t5[:], t5[:], t4[:], ALU.mult, [t5, t4], [t5])
            p.act(t4[:], al[:], AF.Exp, [al], [t4], scale=negm)
            p.ts("dve", t6[:], be[:], mcol, None, ALU.mult, None, [be, g.CT], [t6])
            sincos(p, (t2, t2[:]), (t3, t3[:]), (t6, t6[:]), (ar, ar[:]))
            p.tt("dve", t3[:], t3[:], t4[:], ALU.mult, [t3, t4], [t3])
            p.tt("dve", t2[:], t2[:], t4[:], ALU.mult, [t2, t4], [t2])
            p.ts("dve", t2[:], t2[:], -1.0, None, ALU.mult, None, [t2], [t2])
            sl = slice(q * W, (q + 1) * W)
            p.tt("dve", t4[:], t1[:], t3[:], ALU.mult, [t1, t3], [t4])
            p.tt("dve", t6[:], t5[:], t2[:], ALU.mult, [t5, t2], [t6])
            p.tt("dve", AinT[:, 0, sl], t4[:], t6[:], ALU.subtract, [t4, t6], [AinT])
            p.tt("dve", t4[:], t1[:], t2[:], ALU.mult, [t1, t2], [t4])
            p.tt("dve", t6[:], t5[:], t3[:], ALU.mult, [t5, t3], [t6])
            p.tt("dve", AinT[:, 1, sl], t4[:], t6[:], ALU.add, [t4, t6], [AinT])
        asm = p.sb([128, 3, 32], F32, 'asm')
        for qi, nm in enumerate(('s5_a_re', 's5_a_im')):
            load_T(g, None, I[nm].ap()[l, d].rearrange("(k gs) p -> k (gs p)", gs=2), 32, tmp, bk7)
            p.cp("dve", asm[:, qi, :], bk7[:, 0:32], [bk7], [asm])
        l32 = p.sb([32, 2, 64], F32, 'l32')
        l2 = p.sb([32, 2], F32, 'l2')
        p.dma("sp", l2[:], I['s5_log_step'].ap()[l, d].rearrange("(k gs) -> k gs", gs=2), [], [l2])
        p.act(l2[:], l2[:], AF.Exp, [l2], [l2])
        p.cp("dve", l32[:], l2[:].unsqueeze(2).to_broadcast([32, 2, 64]), [l2], [l32])
        p.tr(bk7[:, 0:32], l32[:].rearrange("p a b -> p (a b)"), g.ident[0:32, 0:32], [l32, g.CT], [bk7])
        p.cp("dve", asm[:, 2, :], bk7[:, 0:32], [bk7], [asm])
        p.tt("dve", asm[:, 0, :], asm[:, 0, :], asm[:, 2, :], ALU.mult, [asm], [asm])
        p.tt("dve", asm[:, 1, :], asm[:, 1, :], asm[:, 2, :], ALU.mult, [asm], [asm])
        A3 = lambda t: t[:].rearrange("p (a b) -> p a b", b=128)
        for q in range(4):
            t1, t2, t3, t4, t6 = tl[0], tl[1], tl[2], tl[3], tl[4]
            ks = slice(q * 8, (q + 1) * 8)
            mb = mrow.unsqueeze(1).to_broadcast([128, 8, 128])
            p.tt("dve", A3(t1), mb, asm[:, 0, ks].unsqueeze(2).to_broadcast([128, 8, 128]), ALU.mult, [g.CT, asm], [t1])
            p.act(t1[:], t1[:], AF.Exp, [t1], [t1])
            p.tt("dve", A3(t6), mb, asm[:, 1, ks].unsqueeze(2).to_broadcast([128, 8, 128]), ALU.mult, [g.CT, asm], [t6])
            sincos(p, (t2, t2[:]), (t3, t3[:]), (t6, t6[:]), (t4, t4[:]))
            p.tt("dve", Aout[:, 0, ks, :], A3(t3), A3(t1), ALU.mult, [t3, t1], [Aout])
            p.tt("dve", Aout[:, 1, ks, :], A3(t2), A3(t1), ALU.mult, [t2, t1], [Aout])
        p.pop()
        p.push()
        tri = g.CB[:, 128:256] if d == 0 else g.CB[:, 256:384]
        uf = [p.sb([128, T], F32, 'uf') for _ in range(1)]
        ub = [p.sb([128, T], BF16, 'ub') for _ in range(2)]
        SBt = [p.sb([128, 2, 512], BF16, 'SBt') for _ in range(2)]
        w4 = [p.sb([128, 512], F32, 'w4_%d' % i) for i in range(4)]
        Xf = [p.sb([128, 2, 128], F32, 'Xf') for _ in range(4)]
        Xb = [p.sb([128, 2, 128], BF16, 'Xb') for _ in range(4)]
        x4 = [p.sb([128, 128], F32, 'x4_%d' % i) for i in range(4)]
        xp = [p.sb([128, 2], F32, 'xprev') for _ in range(4)]
        yo = [p.sb([128, 128], F32, 'yo') for _ in range(2)]
        yf = [p.sb([128, 128], F32, 'yf') for _ in range(2)]
        gb_ = [p.sb([128, 128], BF16, 'gb') for _ in range(2)]
        n2 = 0
        for ct in range(8):
            u_f = uf[0]
            u_b = ub[ct % 2]
            p.dma("sp", u_f[:], g.PFM[1664 + ct * 128:1664 + (ct + 1) * 128, :], [g.PFM], [u_f])
            p.cp("pool", u_b[:], u_f[:], [u_f], [u_b])
            for kk in range(4):
                p.op("dve", lambda e: e.memset(xp[kk][:], 0.0), [], [xp[kk]])
            order = range(NB) if d == 0 else range(NB - 1, -1, -1)
            ccol = 127 if d == 0 else 0
            for c in order:
                cs_ = slice(c * 128, (c + 1) * 128)
                if g.NSEG == 2 and c == (g.SB if d == 0 else g.SB - 1):
                    for kk in range(4):
                        p.ts("dve", xp[kk][:, :], xp[kk][:, :], g.LK[:, 0:1], None, ALU.mult, None, [xp[kk], g.LK], [xp[kk]])
                bre = g.bank[(n2 % 2) * 2]
                bim = g.bank[(n2 % 2) * 2 + 1]
                sbt = SBt[n2 % 2]
                y_o = yo[n2 % 2]
                p.mm(bre[:, :], u_b[:, cs_], Bblk[:, ct, 0, :], True, True, [u_b, Bblk], [bre])
                p.mm(bim[:, :], u_b[:, cs_], Bblk[:, ct, 1, :], True, True, [u_b, Bblk], [bim])
                are = AinT[:, 0, ct * 512:(ct + 1) * 512]
                aim = AinT[:, 1, ct * 512:(ct + 1) * 512]
                p.tt("dve", w4[0][:], bre[:, :], are, ALU.mult, [bre, AinT], [w4[0]])
                p.tt("dve", w4[1][:], bim[:, :], aim, ALU.mult, [bim, AinT], [w4[1]])
                p.tt("pool", sbt[:, 0, :], w4[0][:], w4[1][:], ALU.subtract, [w4[0], w4[1]], [sbt])
                p.tt("dve", w4[2][:], bre[:, :], aim, ALU.mult, [bre, AinT], [w4[2]])
                p.tt("dve", w4[3][:], bim[:, :], are, ALU.mult, [bim, AinT], [w4[3]])
                p.tt("pool", sbt[:, 1, :], w4[2][:], w4[3][:], ALU.add, [w4[2], w4[3]], [sbt])
                ybk = g.bank[6 + n2 % 2]
                for kk in range(4):
                    k = ct * 4 + kk
                    zb = g.bank[4 + kk % 2]
                    xf = Xf[kk]
                    xb = Xb[kk]
                    for ri in range(2):
                        p.mm(zb[:, ri * 128:(ri + 1) * 128], sbt[:, ri, kk * 128:(kk + 1) * 128], tri, True, True,
                             [sbt, g.CB], [zb])
                    zre, zim = zb[:, 0:128], zb[:, 128:256]
                    aor, aoi = Aout[:, 0, k, :], Aout[:, 1, k, :]
                    pre, pim = xp[kk][:, 0:1], xp[kk][:, 1:2]
                    p.stt(x4[0][:], zre, pre, aor, ALU.add, ALU.mult, [zb, xp[kk], Aout], [x4[0]])
                    p.stt(x4[1][:], zim, pim, aoi, ALU.add, ALU.mult, [zb, xp[kk], Aout], [x4[1]])
                    p.stt(x4[2][:], zre, pre, aoi, ALU.add, ALU.mult, [zb, xp[kk], Aout], [x4[2]])
                    p.stt(x4[3][:], zim, pim, aor, ALU.add, ALU.mult, [zb, xp[kk], Aout], [x4[3]])
                    p.tt("pool", xf[:, 0, :], x4[0][:], x4[1][:], ALU.subtract, [x4[0], x4[1]], [xf])
                    p.tt("pool", xf[:, 1, :], x4[2][:], x4[3][:], ALU.add, [x4[2], x4[3]], [xf])
                    p.cp("act", xb[:], xf[:], [xf], [xb])
                    p.cp("act", xp[kk][:, :], xf[:, :, ccol], [xf], [xp[kk]])
                    for ri in range(2):
                        p.mm(ybk[:, 0:128], Cblk[:, k, ri, :], xb[:, ri, :], kk == 0 and ri == 0, kk == 3 and ri == 1,
                             [Cblk, xb], [ybk])
                if d == 0:
                    p.cp("act", y_o[:], ybk[:, 0:128], [ybk], [y_o])
                    p.dma("sp", g.YF[ct * 128:(ct + 1) * 128, cs_], y_o[:], [y_o], [g.YF])
                else:
                    y_f = yf[n2 % 2]
                    gbf = gb_[n2 % 2]
                    p.dma("sp", y_f[:], g.YF[ct * 128:(ct + 1) * 128, cs_], [g.YF], [y_f])
                    p.tt("dve", y_o[:], y_f[:], ybk[:, 0:128], ALU.add, [y_f, ybk], [y_o])
                    p.stt(y_o[:], u_f[:, cs_], dsk[:, ct:ct + 1], y_o[:], ALU.mult, ALU.add, [u_f, dsk, y_o], [y_o])
                    p.tt("dve", y_f[:], y_o[:], y_o[:], ALU.mult, [y_o], [y_f])
                    p.ts("dve", y_f[:], y_f[:], 0.044715, 1.0, ALU.mult, ALU.add, [y_f], [y_f])
                    p.tt("dve", y_f[:], y_f[:], y_o[:], ALU.mult, [y_f, y_o], [y_f])
                    p.act(y_f[:], y_f[:], AF.Sigmoid, [y_f], [y_f], scale=1.5957691216057308)
                    p.tt("dve", y_o[:], y_o[:], y_f[:], ALU.mult, [y_o, y_f], [y_o])
                    p.cp("act", gbf[:], y_o[:], [y_o], [gbf])
                    p.dma("sp", g.YF[ct * 128:(ct + 1) * 128, cs_], y_o[:], [y_o], [g.YF])
                    p.dma("sp", g.QT[ct * 128:(ct + 1) * 128, cs_], gbf[:], [gbf], [g.QT])
                n2 += 1
        p.pop()
    p.push()
    ws = WStream(p)
    ATs = [p.sb([128, 8, 512], BF16, 'AT') for _ in range(2)]
    gfs = [p.sb([128, 512], F32, 'gf') for _ in range(2)]
    sgs = [p.sb([128, 512], F32, 'sg') for _ in range(2)]
    obs = [p.sb([128, 512], BF16, 'ob') for _ in range(2)]
    Wg = I['s5_glu_w'].ap()[l]
    n = 0
    bi = 0
    for s in range(g.NS):
        AT = ATs[s % 2]
        load_AT(g, AT, g.QT, 8, s)
        for cc in range(2):
            banks = g.bank[bi * 4:(bi + 1) * 4]
            bi ^= 1
            gemm_chunk(g, ws, AT, 8, Wg, cc * 512, 512, 'fm', banks)
            for ct in range(4):
                r0 = cc * 512 + ct * 128
                gf, sg, ob = gfs[n % 2], sgs[n % 2], obs[n % 2]
                n += 1
                p.dma("sp", gf[:], g.YF[r0:r0 + 128, s * 512:(s + 1) * 512], [g.YF], [gf])
                p.act(sg[:], banks[ct][:, :], AF.Sigmoid, [banks[ct], glb], [sg], bias=glb[:, cc * 4 + ct:cc * 4 + ct + 1])
                p.tt("dve", ob[:], gf[:], sg[:], ALU.mult, [gf, sg], [ob])
                p.dma("sp", g.OT[2][r0:r0 + 128, s * 512:(s + 1) * 512], ob[:], [ob], [g.OT[2]])
    p.pop()
    p.end_phase()


def layer(g, l, src, dst, stages):
    p = g.p
    mts = g.modT[l]
    norm_phase(g, src, g.G1[l], mts, 0)
    inproj_phase(g, l)
    ssd_phase(g, l)
    swa_phase(g, l)
    s5_phase(g, l)
    na_phase(g, l)
    if g.dbg and l == 0:
        for b in range(4):
            p.dma("sp", g.dbg['ot%d' % b][:, :], g.OT[b][:, :], [g.OT[b]], [g.dbg['ot%d' % b]])
    merge_phase(g, l)
    resid_gemm_phase(g, g.MT, 32, g.I['w_out'].ap()[l], [(mt, mt, 64) for mt in mts], src, g.X1)
    if g.dbg and l == 0:
        p.dma("sp", g.dbg['x1'][:, :], g.X1[:, :], [g.X1], [g.dbg['x1']])
    norm_phase(g, g.X1, g.G2[l], mts, 3)
    ffn1_phase(g, l)
    resid_gemm_phase(g, g.UT, 86, g.I['ffn_w2'].ap()[l], [(mt, mt, 160) for mt in mts], g.X1, dst)


_CACHE = {}


def kernel(**inputs):
    w = {n: np.ascontiguousarray(np.asarray(inputs[n], dtype=np.float32)) for n in WSHAPES}
    xp = np.asarray(inputs['x_prompt'], dtype=np.float32)
    xs = np.asarray(inputs['x_sample'], dtype=np.float32)
    cp_ = np.asarray(inputs['c_prompt'], dtype=np.float32)
    cs = np.asarray(inputs['c_sample'], dtype=np.float32)
    T = xp.shape[1]
    nc = build(T, 2, NSEG=2)
    consts = make_consts()
    rope = make_rope(T)
    maps = []
    for b in range(2):
        lk = np.zeros((128, 2), np.float32)
        lk[:, 0] = 1.0
        m = {'x': np.ascontiguousarray(xp[b]), 'c': np.ascontiguousarray(np.stack([cp_[b], cp_[b]]).reshape(64, 128)),
             'link': lk, 'consts': consts, 'rope': rope}
        m.update(w)
        maps.append(m)
    for b in range(2):
        lk = np.zeros((128, 2), np.float32)
        lk[:, 1] = 1.0
        m = {'x': np.ascontiguousarray(xs[2 * b:2 * b + 2].reshape(T, D)),
             'c': np.ascontiguousarray(cs[2 * b:2 * b + 2].reshape(64, 128)), 'link': lk, 'consts': consts, 'rope': rope}
        m.update(w)
        maps.append(m)
    res = run_bass_kernel_spmd(nc, maps, core_ids=list(range(4))).results
    yp = np.stack([res[0]['y'], res[1]['y']], axis=0).astype(np.float32)
    ys = np.concatenate([res[2]['y'].reshape(2, T // 2, D), res[3]['y'].reshape(2, T // 2, D)], axis=0).astype(np.float32)
    return (yp, ys)
```

```python
import numpy as np
from contextlib import ExitStack
import concourse.bass as bass
import concourse.mybir as mybir

F32 = mybir.dt.float32
BF16 = mybir.dt.bfloat16
AF = mybir.ActivationFunctionType
ALU = mybir.AluOpType
AX = mybir.AxisListType


class Buf:
    __slots__ = ("t", "lw", "rd", "name")

    def __init__(self, t, name):
        self.t = t
        self.lw = None
        self.rd = {}
        self.name = name

    def __getitem__(self, idx):
        return self.t[idx]


class Eng:
    def __init__(self, name, e, sem, sid):
        self.name = name
        self.e = e
        self.sem = sem
        self.sid = sid
        self.count = 0
        self.known = {}


class Slot:
    def __init__(self, sid):
        self.sid = sid
        self.uses = 0


class Prog:
    def __init__(self, nc):
        self.nc = nc
        self.root = ExitStack()
        self.sems = []
        self.eng = {}
        for name, e in (("pe", nc.tensor), ("act", nc.scalar), ("dve", nc.vector),
                        ("pool", nc.gpsimd), ("sp", nc.sync)):
            sem = self.root.enter_context(nc.semaphore("s_" + name))
            self.eng[name] = Eng(name, e, sem, len(self.sems))
            self.sems.append(sem)
        self.slots = {}
        self.rr = {}
        for q, n in (("sp", 8), ("pool", 6), ("act", 4)):
            lst = []
            for i in range(n):
                sem = self.root.enter_context(nc.semaphore("d_%s%d" % (q, i)))
                lst.append(Slot(len(self.sems)))
                self.sems.append(sem)
            self.slots[q] = lst
            self.rr[q] = 0
        self.phase = None
        self.uid = 0

    def begin_phase(self):
        self.phase = ExitStack()
        self.stack = [self.phase]

    def end_phase(self):
        self.barrier()
        assert len(self.stack) == 1
        self.phase.close()
        self.phase = None

    def push(self):
        st = ExitStack()
        self.stack.append(st)
        self.phase = st

    def pop(self):
        self.barrier()
        self.stack.pop().close()
        self.phase = self.stack[-1]

    def sb(self, shape, dtype, name=None, persist=False):
        self.uid += 1
        name = "%s_%d" % (name or "t", self.uid)
        st = self.root if persist else self.phase
        t = st.enter_context(self.nc.sbuf_tensor(name, list(shape), dtype))
        return Buf(t, name)

    def ps(self, shape, dtype, name=None):
        self.uid += 1
        name = "%s_%d" % (name or "p", self.uid)
        t = self.root.enter_context(self.nc.psum_tensor(name, list(shape), dtype))
        return Buf(t, name)

    def dram(self, name, shape, dtype, kind="Internal"):
        h = self.nc.dram_tensor(name, list(shape), dtype, kind=kind)
        b = Buf(h.ap(), name)
        return b

    def _wait(self, E, sid, v):
        if E.known.get(sid, 0) >= v:
            return
        E.e.wait_ge(self.sems[sid], v)
        E.known[sid] = v

    def _deps(self, E, reads, writes, skip_self=False):
        need = {}
        for b in reads:
            if b.lw is not None:
                sid, v = b.lw
                if need.get(sid, 0) < v:
                    need[sid] = v
        for b in writes:
            if b.lw is not None:
                sid, v = b.lw
                if need.get(sid, 0) < v:
                    need[sid] = v
            for sid, v in b.rd.items():
                if need.get(sid, 0) < v:
                    need[sid] = v
        for sid, v in need.items():
            if skip_self and sid == E.sid:
                continue
            self._wait(E, sid, v)

    def _mark(self, ev, reads, writes):
        sid, v = ev
        for b in reads:
            if b.rd.get(sid, 0) < v:
                b.rd[sid] = v
        for b in writes:
            b.lw = ev
            b.rd = {}

    def op(self, en, fn, reads=(), writes=(), ms=True):
        E = self.eng[en]
        self._deps(E, reads, writes, skip_self=(en == "pe"))
        ins = fn(E.e)
        if ms:
            E.count += 1
            ins.then_inc(E.sem, 1)
            ev = (E.sid, E.count)
        else:
            assert en == "pe"
            ev = (E.sid, E.count + 1)
        self._mark(ev, reads, writes)
        return ins

    def dma(self, q, out, in_, reads=(), writes=()):
        E = self.eng[q]
        sl = self.slots[q]
        i = self.rr[q]
        self.rr[q] = (i + 1) % len(sl)
        slot = sl[i]
        self._deps(E, reads, writes)
        if slot.uses > 0:
            self._wait(E, slot.sid, 16 * slot.uses)
        E.e.dma_start(out=out, in_=in_).then_inc(self.sems[slot.sid], 16)
        slot.uses += 1
        self._mark((slot.sid, 16 * slot.uses), reads, writes)

    def barrier(self):
        for E in self.eng.values():
            for F in self.eng.values():
                if F is E or F.count == 0:
                    continue
                self._wait(E, F.sid, F.count)
            for lst in self.slots.values():
                for s in lst:
                    if s.uses:
                        self._wait(E, s.sid, 16 * s.uses)

    def finish(self):
        self.barrier()
        if self.phase is not None:
            self.phase.close()
        self.root.close()

    def mm(self, out, lhsT, rhs, start, stop, reads, writes, ms=None, sgc=False):
        if ms is None:
            ms = stop
        return self.op("pe", lambda e: e.matmul(out, lhsT=lhsT, rhs=rhs, start=start, stop=stop,
                                                skip_group_check=sgc), reads, writes, ms=ms)

    def tr(self, out, in_, ident, reads, writes, ms=True):
        return self.op("pe", lambda e: e.transpose(out, in_, ident), reads, writes, ms=ms)

    def act(self, out, in_, func, reads, writes, bias=None, scale=None, en="act"):
        kw = {}
        if bias is not None:
            kw["bias"] = bias
        if scale is not None:
            kw["scale"] = scale
        return self.op(en, lambda e: e.activation(out=out, in_=in_, func=func, **kw), reads, writes)

    def ts(self, en, out, in0, s1, s2, op0, op1, reads, writes, accum_out=None):
        kw = {}
        if op1 is not None:
            kw["op1"] = op1
        if accum_out is not None:
            kw["accum_out"] = accum_out
        return self.op(en, lambda e: e.tensor_scalar(out=out, in0=in0, scalar1=s1, scalar2=s2, op0=op0, **kw),
                       reads, writes)

    def tt(self, en, out, in0, in1, op, reads, writes):
        return self.op(en, lambda e: e.tensor_tensor(out=out, in0=in0, in1=in1, op=op), reads, writes)

    def stt(self, out, in0, scalar, in1, op0, op1, reads, writes, accum_out=None):
        kw = {}
        if accum_out is not None:
            kw["accum_out"] = accum_out
        return self.op("dve", lambda e: e.scalar_tensor_tensor(out=out, in0=in0, scalar=scalar, in1=in1,
                                                              op0=op0, op1=op1, **kw), reads, writes)

    def cp(self, en, out, in_, reads, writes):
        if en == "act":
            return self.op(en, lambda e: e.activation(out=out, in_=in_, func=AF.Copy), reads, writes)
        return self.op(en, lambda e: e.tensor_copy(out=out, in_=in_), reads, writes)


from concourse.bass_utils import run_bass_kernel_spmd

D = 4096
KT = 32
NIN = 24608
DFF = 11008
EPS = 1e-6
NCONST = 900
PI = float(np.pi)
MAGIC = 12582912.0

WSHAPES = {
    'ada_w': (D, 6 * D), 'ada_b': (6 * D,), 'norm1_g': (D,), 'norm2_g': (D,), 'w_in': (D, NIN),
    'ssd_conv_w': (5, 1536), 'ssd_conv_b': (1536,), 'ssd_dt_bias': (2, 16), 'ssd_a_log': (2, 16),
    'ssd_d': (16,), 'ssd_norm_g': (1024,), 'swa_q_norm_g': (128,), 'swa_k_norm_g': (128,), 'swa_sink': (8,),
    's5_a_re': (2, 64, 64), 's5_a_im': (2, 64, 64), 's5_log_step': (2, 64), 's5_b_re': (64, 64, 16),
    's5_b_im': (64, 64, 16), 's5_c_re': (2, 64, 16, 64), 's5_c_im': (2, 64, 16, 64), 's5_d': (1024,),
    's5_glu_w': (1024, 1024), 's5_glu_b': (1024,), 'na_q_norm_g': (128,), 'na_k_norm_g': (128,),
    'na_rpb': (8, 15, 31), 'w_branch_ssd': (1024, D), 'w_branch_swa': (1024, D), 'w_branch_s5': (1024, D),
    'w_branch_na': (1024, D), 'w_out': (D, D), 'ffn_w1': (D, DFF), 'ffn_w3': (D, DFF), 'ffn_w2': (DFF, D),
}


def make_consts():
    c = np.zeros((128, NCONST), np.float32)
    k = np.arange(128)[:, None]
    q = np.arange(128)[None, :]
    c[:, 0:128] = np.eye(128)
    c[:, 128:256] = (k <= q)
    c[:, 256:384] = (k >= q)
    c[:, 384:512] = 1.0
    c[0:64, 512:576] = np.eye(64)[::-1]
    qc = np.arange(64)
    cs = np.clip(qc - 8, 0, 48)
    kc = np.arange(64)[:, None]
    cv = ((kc >= cs[None, :]) & (kc < cs[None, :] + 16)).astype(np.float32)
    c[0:64, 576:640] = cv
    c[64:128, 576:640] = cv
    s = np.arange(128)
    c[:, 640] = s + 1
    c[:, 641] = 128 - s
    c[:, 642] = -(s + 1)
    c[:, 643] = -(128 - s)
    c[:, 644:772] = (s + 1)[None, :]
    c[:, 772:900] = (128 - s)[None, :]
    return c


def make_rope(T):
    half = 64
    inv = 10000.0 ** (-np.arange(half, dtype=np.float32) / half)
    ang = np.arange(T, dtype=np.float32)[:, None] * inv[None, :]
    return np.concatenate([np.cos(ang), np.sin(ang)], axis=1).astype(np.float32)


class Ctx:
    pass


def build(T, NL, dbg=False, stages=None, NSEG=1):
    NB = T // 128
    NS = T // 512
    nc = bass.Bass("TRN2", target_bir_lowering=False)
    p = Prog(nc)
    g = Ctx()
    g.T, g.NB, g.NS, g.NL, g.nc, g.p = T, NB, NS, NL, nc, p
    g.NSEG = NSEG
    g.SB = NB // NSEG
    I = {}
    I['x'] = nc.dram_tensor('x', [T, D], F32, kind="ExternalInput")
    I['c'] = nc.dram_tensor('c', [NSEG * 32, 128], F32, kind="ExternalInput")
    I['link'] = nc.dram_tensor('link', [128, 2], F32, kind="ExternalInput")
    I['consts'] = nc.dram_tensor('consts', [128, NCONST], F32, kind="ExternalInput")
    I['rope'] = nc.dram_tensor('rope', [T, 128], F32, kind="ExternalInput")
    for n, shp in WSHAPES.items():
        I[n] = nc.dram_tensor(n, [2] + list(shp), F32, kind="ExternalInput")
    g.I = I
    g.y = Buf(nc.dram_tensor('y', [T, D], F32, kind="ExternalOutput").ap(), 'y')
    g.HT = p.dram('HT', [D, T], BF16)
    g.PTM = p.dram('PTM', [T, 5632], F32)
    g.PFM = p.dram('PFM', [2688, T], F32)
    g.OT = [p.dram('OT%d' % b, [1024, T], BF16) for b in range(4)]
    g.MT = p.dram('MT', [D, T], BF16)
    g.X1 = p.dram('X1', [T, D], F32)
    g.X2 = p.dram('X2', [T, D], F32)
    g.UT = p.dram('UT', [DFF, T], BF16)
    g.XS = p.dram('XS', [T, 1024], F32)
    g.RR = p.dram('RR', [32, T], F32)
    g.QT = p.dram('QT', [1024, T], BF16)
    g.KTd = p.dram('KTd', [1024, T], BF16)
    g.YF = p.dram('YF', [1024, T], F32)
    g.RPB = p.dram('RPB', [8 * 15 * 31 + 256], F32)
    g.WC = {'w_in': p.dram('WC_in', [D, NIN], BF16), 'w_out': p.dram('WC_out', [D, D], BF16),
            'w1': p.dram('WC_w1', [D, DFF], BF16), 'w3': p.dram('WC_w3', [D, DFF], BF16),
            'w2': p.dram('WC_w2', [DFF, D], BF16)}
    for b in range(4):
        g.WC['wb%d' % b] = p.dram('WC_wb%d' % b, [1024, D], BF16)
    g.dbg = {}
    if dbg:
        for b in range(4):
            g.dbg['ot%d' % b] = Buf(nc.dram_tensor('dbg_ot%d' % b, [1024, T], BF16, kind="ExternalOutput").ap(), 'dbgot')
        g.dbg['x1'] = Buf(nc.dram_tensor('dbg_x1', [T, D], F32, kind="ExternalOutput").ap(), 'dbgx1')
        g.dbg['ptm'] = Buf(nc.dram_tensor('dbg_ptm', [T, 5632], F32, kind="ExternalOutput").ap(), 'dbgptm')
        g.dbg['pfm'] = Buf(nc.dram_tensor('dbg_pfm', [2688, T], F32, kind="ExternalOutput").ap(), 'dbgpfm')
    g.bank = [p.ps([128, 512], F32, 'bank') for _ in range(8)]
    CT = p.sb([128, NCONST], F32, 'consts', persist=True)
    p.dma("sp", CT[:], I['consts'].ap()[:, :], [], [CT])
    g.CT = CT
    g.ident = CT[:, 0:128]
    g.trile = CT[:, 128:256]
    g.trige = CT[:, 256:384]
    g.ones = CT[:, 384:512]
    CB = p.sb([128, 384], BF16, 'constb', persist=True)
    p.cp("dve", CB[:], CT[:, 0:384], [CT], [CB])
    g.CB = CB
    g.identb = CB[:, 0:128]
    g.modT = [[p.sb([128, 192], F32, 'modT', persist=True) for _ in range(NSEG)] for _ in range(NL)]
    g.G1 = [[p.sb([128, 32], F32, 'G1', persist=True) for _ in range(NSEG)] for _ in range(NL)]
    g.G2 = [[p.sb([128, 32], F32, 'G2', persist=True) for _ in range(NSEG)] for _ in range(NL)]
    g.LK = p.sb([128, 2], F32, 'link', persist=True)
    p.dma("sp", g.LK[:], I['link'].ap()[:, :], [], [g.LK])
    setup_mod(g)
    src = Buf(I['x'].ap(), 'xin')
    for l in range(NL):
        dst = g.y if l == NL - 1 else g.X2
        layer(g, l, src, dst, stages)
        src = g.X2
    p.finish()
    return nc


def evac(p, i, out, in_, reads, writes):
    if i % 2 == 0:
        p.cp("act", out, in_, reads, writes)
    else:
        p.cp("dve", out, in_, reads, writes)


def load_T(g, dst, src_ap, rows, tmp, bank):
    p = g.p
    p.dma("sp", tmp[:rows, :], src_ap, [], [tmp])
    p.tr(bank[:, :rows], tmp[:rows, :], g.ident[:rows, :rows], [tmp, g.CT], [bank])


def setup_mod(g):
    p, I = g.p, g.I
    NSEG = g.NSEG
    p.begin_phase()
    tmp = p.sb([128, 128], F32, 'tmp')
    bk = g.bank[0]
    cT = p.sb([128, 32, NSEG], F32, 'cT')
    for sg in range(NSEG):
        p.dma("sp", tmp[:32, :], I['c'].ap()[sg * 32:(sg + 1) * 32, :], [], [tmp])
        p.act(tmp[:32, :], tmp[:32, :], AF.Silu, [tmp], [tmp])
        p.tr(bk[:, :32], tmp[:32, :], g.ident[:32, :32], [tmp, g.CT], [bk])
        p.cp("dve", cT[:, :, sg], bk[:, :32], [bk], [cT])
    wb = [p.sb([128, 32, 128], F32, 'adaw') for _ in range(3)]
    for l in range(g.NL):
        pm = g.bank[1 + (l % 2)]
        aw = I['ada_w'].ap()[l]
        for j in range(192):
            w = wb[j % 3]
            p.dma("sp", w[:], aw[:, j * 128:(j + 1) * 128].rearrange("(k p) c -> p k c", p=128), [], [w])
            for k in range(32):
                p.mm(pm[:, j * NSEG:(j + 1) * NSEG], w[:, k, :], cT[:, k, :], k == 0, k == 31, [w, cT], [pm])
        ab = I['ada_b'].ap()[l].rearrange("(j p) -> j p", p=128)
        for sg in range(NSEG):
            mt = g.modT[l][sg]
            p.cp("dve", mt[:], pm[:, 0:192 * NSEG].rearrange("p (j s) -> p j s", s=NSEG)[:, :, sg], [pm], [mt])
            load_T(g, None, ab[0:128, :], 128, tmp, bk)
            p.tt("dve", mt[:, 0:128], mt[:, 0:128], bk[:, 0:128], ALU.add, [mt, bk], [mt])
            load_T(g, None, ab[128:192, :], 64, tmp, bk)
            p.tt("dve", mt[:, 128:192], mt[:, 128:192], bk[:, 0:64], ALU.add, [mt, bk], [mt])
            for nm, lst, sc in (('norm1_g', g.G1, 1), ('norm2_g', g.G2, 4)):
                load_T(g, None, I[nm].ap()[l].rearrange("(k p) -> k p", p=128), 32, tmp, bk)
                G = lst[l][sg]
                p.stt(G[:], mt[:, sc * 32:(sc + 1) * 32], 1.0, bk[:, :32], ALU.add, ALU.mult, [mt, bk], [G])
    p.end_phase()


def gate_bc(g, dst, col):
    p = g.p
    gl = [p.sb([128, 128], F32, 'gl') for _ in range(2)]
    for k in range(32):
        t = gl[k % 2]
        p.ts("dve", t[:], g.ones, col[1][:, col[2] + k:col[2] + k + 1], None, ALU.mult, None, [g.CT, col[0]], [t])
        bk = g.bank[(k // 4) % 2]
        p.mm(bk[:, (k % 4) * 128:(k % 4 + 1) * 128], t[:], g.ident, True, True, [t, g.CT], [bk])
        if k % 4 == 3:
            evac(p, k // 4, dst[:, (k // 4) * 512:(k // 4 + 1) * 512], bk[:, :], [bk], [dst])


def norm_phase(g, src, Gs, mts, shc):
    p = g.p
    p.begin_phase()
    xt = [p.sb([128, D], F32, 'xt') for _ in range(2)]
    junk = p.sb([128, D], BF16, 'junk')
    hs = [p.sb([128, 32, 128], BF16, 'hs') for _ in range(2)]
    st = [p.sb([128, 4], F32, 'st') for _ in range(2)]
    for tb in range(g.NB):
        x = xt[tb % 2]
        s = st[tb % 2]
        h = hs[tb % 2]
        G = Gs[tb // g.SB]
        mt = mts[tb // g.SB]
        p.dma("sp", x[:], src[tb * 128:(tb + 1) * 128, :], [src], [x])
        p.stt(junk[:], x[:], 1.0, x[:], ALU.mult, ALU.mult, [x], [junk, s], accum_out=s[:, 0:1])
        p.ts("dve", s[:, 1:2], s[:, 0:1], 1.0 / D, EPS, ALU.mult, ALU.add, [s], [s])
        p.act(s[:, 2:3], s[:, 1:2], AF.Sqrt, [s], [s])
        p.op("dve", lambda e: e.reciprocal(out=s[:, 3:4], in_=s[:, 2:3]), [s], [s])
        p.act(x[:], x[:], AF.Copy, [x, s], [x], scale=s[:, 3:4])
        for k in range(32):
            bk = g.bank[(k // 4) % 4]
            p.tr(bk[:, (k % 4) * 128:(k % 4 + 1) * 128], x[:, k * 128:(k + 1) * 128], g.ident, [x, g.CT], [bk])
            if k % 4 == 3:
                for kk in range(k - 3, k + 1):
                    src_ps = bk[:, (kk % 4) * 128:(kk % 4 + 1) * 128]
                    if kk % 2 == 0:
                        p.act(h[:, kk, :], src_ps, AF.Identity, [bk, G, mt], [h], bias=mt[:, shc * 32 + kk:shc * 32 + kk + 1],
                              scale=G[:, kk:kk + 1])
                    else:
                        p.ts("dve", h[:, kk, :], src_ps, G[:, kk:kk + 1], mt[:, shc * 32 + kk:shc * 32 + kk + 1],
                             ALU.mult, ALU.add, [bk, G, mt], [h])
        p.dma("sp", g.HT[:, tb * 128:(tb + 1) * 128].rearrange("(k p) t -> p k t", p=128), h[:], [h], [g.HT])
    p.end_phase()


class WStream:
    def __init__(self, p, n=3):
        self.bufs = [p.sb([128, 16, 512], BF16, 'wb') for _ in range(n)]
        self.i = 0

    def nxt(self):
        b = self.bufs[self.i % len(self.bufs)]
        self.i += 1
        return b


def gemm_chunk(g, ws, AT, KTn, W, c0, w, mode, banks, cache=None, first=True):
    p = g.p
    npc = (KTn + 15) // 16
    for pc in range(npc):
        k0 = pc * 16
        kn = min(16, KTn - k0)
        wb = ws.nxt()
        if cache is None or first:
            p.dma("pool", wb[:, :kn, :w], W[k0 * 128:(k0 + kn) * 128, c0:c0 + w].rearrange("(k p) c -> p k c", p=128),
                  [], [wb])
            if cache is not None:
                p.dma("sp", cache[k0 * 128:(k0 + kn) * 128, c0:c0 + w].rearrange("(k p) c -> p k c", p=128),
                      wb[:, :kn, :w], [wb], [cache])
        else:
            p.dma("pool", wb[:, :kn, :w], cache[k0 * 128:(k0 + kn) * 128, c0:c0 + w].rearrange("(k p) c -> p k c", p=128),
                  [cache], [wb])
        if mode == 'tm':
            for tb in range(4):
                for k in range(kn):
                    kk = k0 + k
                    last = (kk == KTn - 1)
                    p.mm(banks[tb][:, :w], AT[:, kk, tb * 128:(tb + 1) * 128], wb[:, k, :w], kk == 0, last,
                         [AT, wb], [banks[tb]], ms=(last or (tb == 3 and k == kn - 1)))
        else:
            nct = (w + 127) // 128
            for ct in range(nct):
                cw = min(128, w - ct * 128)
                for k in range(kn):
                    kk = k0 + k
                    last = (kk == KTn - 1)
                    p.mm(banks[ct][:cw, :], wb[:, k, ct * 128:ct * 128 + cw], AT[:, kk, :], kk == 0, last,
                         [AT, wb], [banks[ct]], ms=(last or (ct == nct - 1 and k == kn - 1)))


def load_AT(g, dst, srcT, KTn, s):
    g.p.dma("sp", dst[:, :KTn, :], srcT[:KTn * 128, s * 512:(s + 1) * 512].rearrange("(k p) t -> p k t", p=128),
            [srcT], [dst])


def inproj_phase(g, l):
    p = g.p
    p.begin_phase()
    W = g.I['w_in'].ap()[l]
    ws = WStream(p)
    ATs = [p.sb([128, 32, 512], BF16, 'AT') for _ in range(2)]
    stg = [p.sb([128, 512], F32, 'stg') for _ in range(4)]
    si = 0
    jobs = []
    for (c0, n, d0) in ((0, 1024, 0), (2592, 1536, 1024), (5152, 3072, 2560)):
        for o in range(0, n, 512):
            jobs.append((c0 + o, 512, 'tm', d0 + o))
    for (c0, n, d0) in ((1024, 1536, 0), (2560, 32, 1536), (4128, 1024, 1664)):
        for o in range(0, n, 512):
            jobs.append((c0 + o, min(512, n - o), 'fm', d0 + o))
    bi = 0
    for s in range(g.NS):
        AT = ATs[s % 2]
        load_AT(g, AT, g.HT, 32, s)
        for (c0, w, mode, d0) in jobs:
            banks = g.bank[bi * 4:(bi + 1) * 4]
            bi ^= 1
            gemm_chunk(g, ws, AT, 32, W, c0, w, mode, banks, g.WC['w_in'], s == 0)
            if mode == 'tm':
                for tb in range(4):
                    sg = stg[si % 4]
                    evac(p, si, sg[:, :w], banks[tb][:, :w], [banks[tb]], [sg])
                    si += 1
                    p.dma("sp", g.PTM[(s * 4 + tb) * 128:(s * 4 + tb + 1) * 128, d0:d0 + w], sg[:, :w], [sg], [g.PTM])
            else:
                for ct in range((w + 127) // 128):
                    cw = min(128, w - ct * 128)
                    sg = stg[si % 4]
                    evac(p, si, sg[:cw, :], banks[ct][:cw, :], [banks[ct]], [sg])
                    si += 1
                    p.dma("sp", g.PFM[d0 + ct * 128:d0 + ct * 128 + cw, s * 512:(s + 1) * 512], sg[:cw, :], [sg], [g.PFM])
    p.end_phase()
    if 'ptm' in g.dbg and l == 0:
        p.dma("sp", g.dbg['ptm'][:, :], g.PTM[:, :], [g.PTM], [g.dbg['ptm']])
        p.dma("sp", g.dbg['pfm'][:, :], g.PFM[:, :], [g.PFM], [g.dbg['pfm']])


def merge_phase(g, l):
    p = g.p
    p.begin_phase()
    W = g.I['w_in'].ap()[l]
    WB = [g.I[n].ap()[l] for n in ('w_branch_ssd', 'w_branch_swa', 'w_branch_s5', 'w_branch_na')]
    ws = WStream(p)
    ATs = [p.sb([128, 32, 512], BF16, 'AT') for _ in range(2)]
    OTs = [[p.sb([128, 8, 512], BF16, 'OTs') for _ in range(4)] for _ in range(1)]
    acc = [p.sb([128, 512], F32, 'acc') for _ in range(4)]
    sig = [p.sb([128, 512], F32, 'sig') for _ in range(2)]
    outb = [p.sb([128, 512], BF16, 'outb') for _ in range(2)]
    n = 0
    for s in range(g.NS):
        AT = ATs[s % 2]
        load_AT(g, AT, g.HT, 32, s)
        for b in range(4):
            load_AT(g, OTs[0][b], g.OT[b], 8, s)
        for cc in range(8):
            for b in range(4):
                gb = g.bank[0:4]
                yb = g.bank[4:8]
                gemm_chunk(g, ws, AT, 32, W, 8224 + b * D + cc * 512, 512, 'fm', gb, g.WC['w_in'], s == 0)
                gemm_chunk(g, ws, OTs[0][b], 8, WB[b], cc * 512, 512, 'fm', yb, g.WC['wb%d' % b], s == 0)
                for ct in range(4):
                    sg = sig[n % 2]
                    n += 1
                    p.act(sg[:], gb[ct][:, :], AF.Sigmoid, [gb[ct]], [sg])
                    if b == 0:
                        p.tt("dve", acc[ct][:], sg[:], yb[ct][:, :], ALU.mult, [sg, yb[ct]], [acc[ct]])
                    else:
                        p.tt("dve", sg[:], sg[:], yb[ct][:, :], ALU.mult, [sg, yb[ct]], [sg])
                        if b < 3:
                            p.tt("pool", acc[ct][:], acc[ct][:], sg[:], ALU.add, [acc[ct], sg], [acc[ct]])
                        else:
                            ob = outb[ct % 2]
                            p.tt("pool", ob[:], acc[ct][:], sg[:], ALU.add, [acc[ct], sg], [ob])
                            r0 = cc * 512 + ct * 128
                            p.dma("sp", g.MT[r0:r0 + 128, s * 512:(s + 1) * 512], ob[:], [ob], [g.MT])
    p.end_phase()


def resid_gemm_phase(g, srcT, KTn, W, gcols, xin, xout, cache=None):
    p = g.p
    p.begin_phase()
    ws = WStream(p)
    ATs = [p.sb([128, KTn, 512], BF16, 'AT') for _ in range(2 if KTn <= 32 else 1)]
    gbcs = []
    for gcol in gcols:
        gb1 = p.sb([128, D], F32, 'gbc')
        gate_bc(g, gb1, gcol)
        gbcs.append(gb1)
    xr = [p.sb([128, 512], F32, 'xr') for _ in range(4)]
    n = 0
    bi = 0
    for s in range(g.NS):
        AT = ATs[s % len(ATs)]
        load_AT(g, AT, srcT, KTn, s)
        for cc in range(8):
            banks = g.bank[bi * 4:(bi + 1) * 4]
            bi ^= 1
            gemm_chunk(g, ws, AT, KTn, W, cc * 512, 512, 'tm', banks, cache, s == 0)
            for tb in range(4):
                r0 = (s * 4 + tb) * 128
                gbc = gbcs[(s * 4 + tb) // g.SB]
                x = xr[n % 4]
                n += 1
                p.dma("sp", x[:], xin[r0:r0 + 128, cc * 512:(cc + 1) * 512], [xin], [x])
                t = p.tt("dve", banks[tb][:, :], banks[tb][:, :], gbc[:, cc * 512:(cc + 1) * 512], ALU.mult,
                         [banks[tb], gbc], [banks[tb]])
                p.tt("dve", x[:], x[:], banks[tb][:, :], ALU.add, [x, banks[tb]], [x])
                p.dma("sp", xout[r0:r0 + 128, cc * 512:(cc + 1) * 512], x[:], [x], [xout])
    p.end_phase()


def ffn1_phase(g, l):
    p = g.p
    p.begin_phase()
    W1 = g.I['ffn_w1'].ap()[l]
    W3 = g.I['ffn_w3'].ap()[l]
    ws = WStream(p)
    ATs = [p.sb([128, 32, 512], BF16, 'AT') for _ in range(2)]
    sl = [p.sb([128, 512], F32, 'sl') for _ in range(2)]
    ub = [p.sb([128, 512], BF16, 'ub') for _ in range(2)]
    n = 0
    for s in range(g.NS):
        AT = ATs[s % 2]
        load_AT(g, AT, g.HT, 32, s)
        for cc in range(22):
            w = 512 if cc < 21 else 256
            a = g.bank[0:4]
            b = g.bank[4:8]
            gemm_chunk(g, ws, AT, 32, W1, cc * 512, w, 'fm', a, g.WC['w1'], s == 0)
            gemm_chunk(g, ws, AT, 32, W3, cc * 512, w, 'fm', b, g.WC['w3'], s == 0)
            for ct in range(w // 128):
                sg = sl[n % 2]
                u = ub[n % 2]
                n += 1
                p.act(sg[:], a[ct][:, :], AF.Silu, [a[ct]], [sg])
                p.tt("dve", u[:], sg[:], b[ct][:, :], ALU.mult, [sg, b[ct]], [u])
                r0 = cc * 512 + ct * 128
                p.dma("sp", g.UT[r0:r0 + 128, s * 512:(s + 1) * 512], u[:], [u], [g.UT])
    p.end_phase()


def attn_prepass(g, l, qc0, kc0, nk, gqn, gkn, rope):
    p, I = g.p, g.I
    nh = 8 + nk
    gq = p.sb([128, 128], F32, 'gq')
    gk = p.sb([128, 128], F32, 'gk')
    p.dma("sp", gq[:], bass.AP(I[gqn], l * 128, [[0, 128], [1, 128]]), [], [gq])
    p.dma("sp", gk[:], bass.AP(I[gkn], l * 128, [[0, 128], [1, 128]]), [], [gk])
    xs = [p.sb([128, nh, 128], F32, 'qk') for _ in range(2)]
    t1 = p.sb([128, nh, 128], F32, 'qkt')
    t2 = p.sb([128, nh, 64], F32, 'qkt2')
    t3 = p.sb([128, nh, 64], F32, 'qkt3')
    rp = [p.sb([128, 128], F32, 'rp') for _ in range(2)]
    st = [p.sb([128, 3, nh], F32, 'qst') for _ in range(2)]
    ob = [p.sb([128, nh, 128], BF16, 'qkb') for _ in range(2)]
    for tb in range(g.NB):
        x = xs[tb % 2]
        s = st[tb % 2]
        o = ob[tb % 2]
        r0 = tb * 128
        p.dma("sp", x[:, 0:8, :], g.PTM[r0:r0 + 128, qc0:qc0 + 1024].rearrange("p (h d) -> p h d", d=128), [g.PTM], [x])
        p.dma("sp", x[:, 8:nh, :], g.PTM[r0:r0 + 128, kc0:kc0 + nk * 128].rearrange("p (h d) -> p h d", d=128), [g.PTM], [x])
        p.tt("dve", t1[:], x[:], x[:], ALU.mult, [x], [t1])
        p.op("dve", lambda e: e.tensor_reduce(out=s[:, 0, :], in_=t1[:], axis=AX.X, op=ALU.add), [t1], [s])
        p.ts("dve", s[:, 1, :], s[:, 0, :], 1.0 / 128, EPS, ALU.mult, ALU.add, [s], [s])
        p.act(s[:, 2, :], s[:, 1, :], AF.Sqrt, [s], [s])
        p.op("dve", lambda e: e.reciprocal(out=s[:, 0, :], in_=s[:, 2, :]), [s], [s])
        p.tt("dve", x[:], x[:], s[:, 0, :].unsqueeze(2).to_broadcast([128, nh, 128]), ALU.mult, [x, s], [x])
        p.tt("pool", x[:, 0:8, :], x[:, 0:8, :], gq[:].unsqueeze(1).to_broadcast([128, 8, 128]), ALU.mult, [x, gq], [x])
        p.tt("pool", x[:, 8:nh, :], x[:, 8:nh, :], gk[:].unsqueeze(1).to_broadcast([128, nk, 128]), ALU.mult, [x, gk], [x])
        if rope:
            r = rp[tb % 2]
            p.dma("sp", r[:], I['rope'].ap()[r0:r0 + 128, :], [], [r])
            cs = r[:, 0:64].unsqueeze(1).to_broadcast([128, nh, 64])
            sn = r[:, 64:128].unsqueeze(1).to_broadcast([128, nh, 64])
            x1 = x[:, :, 0:64]
            x2 = x[:, :, 64:128]
            p.tt("dve", t2[:], x2, sn, ALU.mult, [x, r], [t2])
            p.tt("dve", t3[:], x1, sn, ALU.mult, [x, r], [t3])
            p.tt("dve", t1[:, :, 0:64], x1, cs, ALU.mult, [x, r], [t1])
            p.tt("dve", t1[:, :, 64:128], x2, cs, ALU.mult, [x, r], [t1])
            p.tt("dve", x[:, :, 0:64], t1[:, :, 0:64], t2[:], ALU.subtract, [t1, t2], [x])
            p.tt("dve", x[:, :, 64:128], t1[:, :, 64:128], t3[:], ALU.add, [t1, t3], [x])
        for hh in range(nh):
            bk = g.bank[(hh // 4) % 4]
            p.tr(bk[:, (hh % 4) * 128:(hh % 4 + 1) * 128], x[:, hh, :], g.ident, [x, g.CT], [bk])
            if hh % 4 == 3 or hh == nh - 1:
                h0 = (hh // 4) * 4
                n = hh - h0 + 1
                evac(p, hh // 4, o[:, h0:h0 + n, :], bk[:, 0:n * 128].rearrange("p (h t) -> p h t", t=128), [bk], [o])
        p.dma("sp", g.QT[:, r0:r0 + 128].rearrange("(h p) t -> p h t", p=128), o[:, 0:8, :], [o], [g.QT])
        p.dma("sp", g.KTd[0:nk * 128, r0:r0 + 128].rearrange("(h p) t -> p h t", p=128), o[:, 8:nh, :], [o], [g.KTd])


def attn_main(g, l, b, nk, vc0, jlist, maskfn, esink):
    p = g.p
    T, NB = g.T, g.NB
    grp = 8 // nk
    qT = [p.sb([128, T], BF16, 'qT') for _ in range(2)]
    kT = [p.sb([128, T], BF16, 'kT') for _ in range(2)]
    vf = p.sb([128, NB, 128], F32, 'vf')
    va = [p.sb([128, NB, 129], BF16, 'va') for _ in range(2)]
    oT = [p.sb([128, T], BF16, 'oT') for _ in range(2)]
    PT = [p.sb([128, 8, 128], F32, 'PT') for _ in range(2)]
    PTb = [p.sb([128, 8, 128], BF16, 'PTb') for _ in range(2)]
    otm = [p.sb([128, 128], F32, 'otm') for _ in range(2)]
    dn = [p.sb([128, 2], F32, 'dn') for _ in range(2)]
    scale = 128.0 ** -0.5
    n = 0
    for h in range(8):
        kvh = h // grp
        q = qT[h % 2]
        k = kT[h % 2]
        v = va[h % 2]
        o = oT[h % 2]
        p.dma("sp", q[:], g.QT[h * 128:(h + 1) * 128, :], [g.QT], [q])
        p.dma("sp", k[:], g.KTd[kvh * 128:(kvh + 1) * 128, :], [g.KTd], [k])
        p.dma("sp", vf[:], g.PTM[:, vc0 + kvh * 128:vc0 + (kvh + 1) * 128].rearrange("(b p) d -> p b d", p=128), [g.PTM], [vf])
        p.cp("pool", v[:, :, 0:128], vf[:], [vf], [v])
        p.op("pool", lambda e: e.memset(v[:, :, 128:129], 1.0), [], [v])
        for i in range(NB):
            js = jlist(i)
            nj = len(js)
            pt = PT[n % 2]
            ptb = PTb[n % 2]
            sb = [g.bank[(n % 2) * 2], g.bank[(n % 2) * 2 + 1]]
            obk = g.bank[4 + n % 2]
            tbk = g.bank[6 + n % 2]
            om = otm[n % 2]
            d = dn[n % 2]
            n += 1
            for idx, j in enumerate(js):
                bk = sb[idx // 4]
                p.mm(bk[:, (idx % 4) * 128:(idx % 4 + 1) * 128], k[:, j * 128:(j + 1) * 128], q[:, i * 128:(i + 1) * 128],
                     True, True, [k, q], [bk])
            for half in range((nj + 3) // 4):
                cnt = min(4, nj - half * 4)
                p.act(pt[:, half * 4:half * 4 + cnt, :], sb[half][:, 0:cnt * 128].rearrange("p (j t) -> p j t", t=128),
                      AF.Exp, [sb[half]], [pt], scale=scale)
            for idx, j in enumerate(js):
                if not maskfn(i, j, h, pt, ptb, idx):
                    p.cp("pool", ptb[:, idx, :], pt[:, idx, :], [pt], [ptb])
            for idx, j in enumerate(js):
                p.mm(obk[:, 0:129], ptb[:, idx, :], v[:, j, :], idx == 0, idx == nj - 1, [ptb, v], [obk])
            if esink is not None:
                p.tt("dve", d[:, 0:1], obk[:, 128:129], esink[:, h:h + 1], ALU.add, [obk, esink], [d])
                p.op("dve", lambda e: e.reciprocal(out=d[:, 1:2], in_=d[:, 0:1]), [d], [d])
            else:
                p.op("dve", lambda e: e.reciprocal(out=d[:, 1:2], in_=obk[:, 128:129]), [obk], [d])
            p.act(om[:], obk[:, 0:128], AF.Copy, [obk, d], [om], scale=d[:, 1:2])
            p.tr(tbk[:, 0:128], om[:], g.ident, [om, g.CT], [tbk])
            p.cp("dve", o[:, i * 128:(i + 1) * 128], tbk[:, 0:128], [tbk], [o])
        p.dma("sp", g.OT[b][h * 128:(h + 1) * 128, :], o[:], [o], [g.OT[b]])


def swa_phase(g, l):
    p, I = g.p, g.I
    p.begin_phase()
    attn_prepass(g, l, 1024, 2048, 2, 'swa_q_norm_g', 'swa_k_norm_g', True)
    p.end_phase()
    p.begin_phase()
    es = p.sb([128, 8], F32, 'esink')
    p.dma("sp", es[:], bass.AP(I['swa_sink'], l * 8, [[0, 128], [1, 8]]), [], [es])
    p.act(es[:], es[:], AF.Exp, [es], [es])
    NB = g.NB

    def jlist(i):
        return [j for j in (i - 1, i, i + 1) if 0 <= j < NB]

    def maskfn(i, j, h, pt, ptb, idx):
        if j == i:
            return False
        m = g.trige if j < i else g.trile
        if (i // g.SB) != (j // g.SB):
            p.stt(ptb[:, idx, :], pt[:, idx, :], g.LK[:, 0:1], m, ALU.mult, ALU.mult, [pt, g.CT, g.LK], [ptb])
        else:
            p.tt("dve", ptb[:, idx, :], pt[:, idx, :], m, ALU.mult, [pt, g.CT], [ptb])
        return True

    attn_main(g, l, 1, 2, 2304, jlist, maskfn, es)
    p.end_phase()


def na_phase(g, l):
    p, I = g.p, g.I
    p.begin_phase()
    attn_prepass(g, l, 2560, 3584, 8, 'na_q_norm_g', 'na_k_norm_g', False)
    p.end_phase()
    p.begin_phase()
    R = g.T // 64
    z = p.sb([8, 465], F32, 'rpbt')
    zz = p.sb([1, 128], F32, 'zz')
    p.op("dve", lambda e: e.memset(zz[:], 0.0), [], [zz])
    p.dma("sp", z[:], I['na_rpb'].ap()[l].rearrange("h a b -> h (a b)"), [], [z])
    p.dma("sp", g.RPB[0:128].rearrange("(o n) -> o n", o=1), zz[:], [zz], [g.RPB])
    p.dma("sp", g.RPB[128 + 3720:128 + 3720 + 128].rearrange("(o n) -> o n", o=1), zz[:], [zz], [g.RPB])
    p.dma("sp", g.RPB[128:128 + 3720].rearrange("(h n) -> h n", h=8), z[:], [z], [g.RPB])
    EB = p.sb([128, 8, 15, 64], F32, 'EB')
    hk = [p.sb([64, 15, 64], F32, 'hk') for _ in range(2)]
    hd = [p.sb([64, 15, 2, 64], F32, 'hd') for _ in range(2)]
    cv = g.CT[:, 576:640]
    J = g.CT[0:64, 512:576]
    for h in range(8):
        a = hk[h % 2]
        d2 = hd[h % 2]
        p.dma("sp", a[:], bass.AP(g.RPB.t.tensor, 128 + h * 465 - 48, [[1, 64], [31, 15], [1, 64]]), [g.RPB], [a])
        p.act(d2[:, :, 0, :], a[:], AF.Exp, [a], [d2])
        p.cp("dve", d2[:, :, 1, :], d2[:, :, 0, :], [d2], [d2])
        for half in range(2):
            bk = g.bank[(h * 2 + half) % 4]
            cnt = 8 if half == 0 else 7
            for dd in range(cnt):
                dri = half * 8 + dd
                p.mm(bk[:, dd * 64:(dd + 1) * 64], d2[:, dri, :, :].rearrange("p a b -> p (a b)"), J, True, True, [d2, g.CT], [bk])
            p.tt("dve", EB[:, h, half * 8:half * 8 + cnt, :], bk[:, 0:cnt * 64].rearrange("p (a b) -> p a b", b=64),
                 cv.unsqueeze(1).to_broadcast([128, cnt, 64]), ALU.mult, [bk, g.CT], [EB])

    Rs = R // g.NSEG

    def quad(i, j):
        res = []
        for kq in range(2):
            for qq in range(2):
                qr, kr = 2 * i + qq, 2 * j + kq
                rl = min(max(qr - 4, 0), R - 8)
                okL = rl <= kr <= rl + 7
                base = (qr // Rs) * Rs
                ru = base + min(max(qr - base - 4, 0), Rs - 8)
                okU = ru <= kr <= ru + 7
                if g.NSEG == 1:
                    code = 1 if okL else 0
                else:
                    code = 1 if (okL and okU) else (2 if okL else (3 if okU else 0))
                res.append((kq, qq, code, kr - qr + 7))
        return res

    def jlist(i):
        return [j for j in range(max(0, i - 4), min(g.NB, i + 5)) if any(q[2] for q in quad(i, j))]

    def maskfn(i, j, h, pt, ptb, idx):
        for (kq, qq, code, dri) in quad(i, j):
            ps_ = slice(kq * 64, (kq + 1) * 64)
            fs = slice(qq * 64, (qq + 1) * 64)
            if code == 1:
                p.tt("dve", ptb[ps_, idx, fs], pt[ps_, idx, fs], EB[ps_, h, dri, :], ALU.mult, [pt, EB], [ptb])
            elif code == 0:
                p.op("pool", lambda e: e.memset(ptb[ps_, idx, fs], 0.0), [], [ptb])
            else:
                lc = g.LK[ps_, 0:1] if code == 2 else g.LK[ps_, 1:2]
                p.stt(ptb[ps_, idx, fs], pt[ps_, idx, fs], lc, EB[ps_, h, dri, :], ALU.mult, ALU.mult, [pt, EB, g.LK], [ptb])
        return True

    attn_main(g, l, 3, 8, 4608, jlist, maskfn, None)
    p.end_phase()


def bc_rows(g, name, l, n, width=None):
    p = g.p
    t = p.sb([128, n], F32, 'bc')
    p.dma("sp", t[:], bass.AP(g.I[name], l * n, [[0, 128], [1, n]]), [], [t])
    return t


def ssd_phase(g, l):
    p, I = g.p, g.I
    T, NB = g.T, g.NB
    p.begin_phase()
    bk0 = g.bank[7]
    cwt = p.sb([6, 1536], F32, 'cwt')
    p.dma("sp", cwt[0:5, :], I['ssd_conv_w'].ap()[l], [], [cwt])
    p.dma("sp", cwt[5:6, :], I['ssd_conv_b'].ap()[l].rearrange("(o n) -> o n", o=1), [], [cwt])
    cw = p.sb([128, 12, 8], F32, 'cw')
    for c in range(12):
        p.tr(bk0[:, c * 8:c * 8 + 6], cwt[0:6, c * 128:(c + 1) * 128], g.ident[0:6, 0:6], [cwt, g.CT], [bk0])
    p.cp("dve", cw[:], bk0[:, 0:96].rearrange("p (c j) -> p c j", j=8), [bk0], [cw])
    TM = p.sb([128, NB, 2, 3, 16], F32, 'TM')
    p.push()
    onesT = p.sb([16, T], F32, 'onesT')
    p.op("dve", lambda e: e.memset(onesT[:], 1.0), [], [onesT])
    for d in range(2):
        raw = p.sb([16, T], F32, 'raw')
        w1 = p.sb([16, T], F32, 'w1')
        w2 = p.sb([16, T], F32, 'w2')
        R = p.sb([16, T], F32, 'R')
        col = p.sb([16, 2], F32, 'col')
        p.dma("sp", raw[:], g.PFM[1536 + 16 * d:1552 + 16 * d, :], [g.PFM], [raw])
        p.dma("sp", col[:, 0:1], bass.AP(I['ssd_dt_bias'], l * 32 + d * 16, [[1, 16], [1, 1]]), [], [col])
        p.dma("sp", col[:, 1:2], bass.AP(I['ssd_a_log'], l * 32 + d * 16, [[1, 16], [1, 1]]), [], [col])
        p.act(col[:, 1:2], col[:, 1:2], AF.Exp, [col], [col])
        p.ts("dve", col[:, 1:2], col[:, 1:2], -1.0, None, ALU.mult, None, [col], [col])
        p.ts("dve", raw[:], raw[:], col[:, 0:1], None, ALU.add, None, [raw, col], [raw])
        p.act(w1[:], raw[:], AF.Abs, [raw], [w1])
        p.act(w1[:], w1[:], AF.Exp, [w1], [w1], scale=-1.0)
        p.act(w1[:], w1[:], AF.Ln, [w1], [w1], bias=1.0)
        p.stt(w1[:], raw[:], 0.0, w1[:], ALU.max, ALU.add, [raw, w1], [w1])
        p.act(w2[:], w1[:], AF.Ln, [w1], [w2])
        p.ts("dve", w1[:], w1[:], col[:, 1:2], None, ALU.mult, None, [w1, col], [w1])
        p.op("dve", lambda e: e.tensor_tensor_scan(out=R[:], data0=onesT[:], data1=w1[:], initial=0.0,
                                                   op0=ALU.mult, op1=ALU.add), [onesT, w1], [R])
        if d == 1:
            p.tt("dve", R[:], R[:], w1[:], ALU.subtract, [R, w1], [R])
            p.tt("dve", w1[:], w2[:], R[:], ALU.add, [w2, R], [w1])
        else:
            p.tt("dve", w1[:], w2[:], R[:], ALU.subtract, [w2, R], [w1])
        for tb in range(NB):
            bk = g.bank[4 + tb % 2]
            for qi, src in enumerate((R, w2, w1)):
                p.tr(bk[:, qi * 16:(qi + 1) * 16], src[:, tb * 128:(tb + 1) * 128], g.ident[0:16, 0:16], [src, g.CT], [bk])
            p.cp("act" if tb % 2 else "dve", TM[:, tb, d, :, :], bk[:, 0:48].rearrange("p (q h) -> p q h", h=16), [bk], [TM])
        p.dma("sp", g.RR[16 * d:16 * d + 16, :], R[:], [R], [g.RR])
    p.pop()
    BT = p.sb([128, 2, T], BF16, 'BT')
    CTt = p.sb([128, 2, T], BF16, 'CTt')
    xsb = p.sb([128, NB, 1024], BF16, 'xsb')
    p.push()
    NSEG = g.NSEG
    H = T // NSEG
    XW = T + 4 * NSEG
    xps = [p.sb([128, XW], F32, 'xp') for _ in range(1)]
    accs = [p.sb([128, T], F32, 'cacc') for _ in range(1)]
    stg = [p.sb([128, 4, 128], F32, 'cstg') for _ in range(2)]
    for xp in xps:
        p.op("dve", lambda e: e.memset(xp[:, 0:2], 0.0), [], [xp])
        p.op("dve", lambda e: e.memset(xp[:, XW - 2:XW], 0.0), [], [xp])
    n = 0
    for c in range(12):
        xp = xps[0]
        acc = accs[0]
        for sg in range(NSEG):
            b0 = 2 + sg * (H + 4)
            p.dma("sp", xp[:, b0:b0 + H], g.PFM[c * 128:(c + 1) * 128, sg * H:(sg + 1) * H], [g.PFM], [xp])
        if NSEG == 2:
            p.ts("dve", xp[:, 2 + H:4 + H], xp[:, 6 + H:8 + H], g.LK[:, 0:1], None, ALU.mult, None, [xp, g.LK], [xp])
            p.ts("dve", xp[:, 4 + H:6 + H], xp[:, H:2 + H], g.LK[:, 0:1], None, ALU.mult, None, [xp, g.LK], [xp])
        for sg in range(NSEG):
            w0 = sg * (H + 4)
            a = acc[:, sg * H:(sg + 1) * H]
            p.ts("dve", a, xp[:, w0:w0 + H], cw[:, c, 0:1], None, ALU.mult, None, [xp, cw], [acc])
            for jj in range(1, 5):
                p.stt(a, xp[:, w0 + jj:w0 + jj + H], cw[:, c, jj:jj + 1], a, ALU.mult, ALU.add, [xp, cw, acc], [acc])
        if c >= 8:
            dst = BT if c < 10 else CTt
            p.act(dst[:, c % 2, :], acc[:], AF.Silu, [acc, cw], [dst], bias=cw[:, c, 5:6])
        else:
            p.act(acc[:], acc[:], AF.Silu, [acc, cw], [acc], bias=cw[:, c, 5:6])
            for q4 in range(NB // 4):
                bk = g.bank[n % 4]
                sg = stg[n % 2]
                n += 1
                for t4 in range(4):
                    tb = q4 * 4 + t4
                    p.tr(bk[:, t4 * 128:(t4 + 1) * 128], acc[:, tb * 128:(tb + 1) * 128], g.ident, [acc, g.CT], [bk])
                p.cp("act", sg[:], bk[:, :].rearrange("p (b c) -> p b c", c=128), [bk], [sg])
                p.cp("pool", xsb[:, q4 * 4:q4 * 4 + 4, c * 128:(c + 1) * 128], sg[:], [sg], [xsb])
                p.dma("sp", g.XS[q4 * 512:(q4 + 1) * 512, c * 128:(c + 1) * 128].rearrange("(b p) c -> p b c", p=128),
                      sg[:], [sg], [g.XS])
    p.pop()
    p.push()
    SELS = p.sb([16, 16, 128], F32, 'SELS')
    for h in range(16):
        p.cp("pool", SELS[:, h, :], g.ident[0:16, h:h + 1].to_broadcast([16, 128]), [g.CT], [SELS])
    d16 = bc_rows(g, 'ssd_d', l, 16)
    Dbc = p.sb([128, 16, 64], F32, 'Dbc')
    p.cp("dve", Dbc[:], d16[:].unsqueeze(2).to_broadcast([128, 16, 64]), [d16], [Dbc])
    NG = bc_rows(g, 'ssd_norm_g', l, 1024)
    RBi = [p.sb([128, 32, 128], F32, 'RBi') for _ in range(1)]
    Ri = [p.sb([16, 2, 128], F32, 'Ri') for _ in range(2)]
    Gm = [p.sb([128, 2, 2, 128], F32, 'Gm') for _ in range(2)]
    Pt = [p.sb([128, 128], F32, 'Pt') for _ in range(4)]
    Mt = [p.sb([128, 128], BF16, 'Mt') for _ in range(4)]
    zt = [p.sb([128, 1024], F32, 'zt') for _ in range(1)]
    xsi = [p.sb([128, 1024], F32, 'xsi') for _ in range(1)]
    yt = [p.sb([128, 1024], F32, 'yt') for _ in range(1)]
    junk = p.sb([128, 1024], BF16, 'junk')
    sst = [p.sb([128, 4], F32, 'sst') for _ in range(2)]
    osb = [p.sb([128, 8, 128], BF16, 'osb') for _ in range(2)]
    n = 0
    gn = 0
    for i in range(NB):
        rb = RBi[0]
        z = zt[0]
        xs = xsi[0]
        y = yt[0]
        ri = Ri[i % 2]
        p.dma("sp", ri[:], g.RR[:, i * 128:(i + 1) * 128].rearrange("(d h) t -> h d t", d=2), [g.RR], [ri])
        s = sst[i % 2]
        ob = osb[i % 2]
        yb = [g.bank[0], g.bank[1]]
        p.dma("sp", z[:], g.PTM[i * 128:(i + 1) * 128, 0:1024], [g.PTM], [z])
        p.dma("sp", xs[:], g.XS[i * 128:(i + 1) * 128, :], [g.XS], [xs])
        for hd4 in range(8):
            bk = g.bank[4 + hd4 % 4]
            for q in range(4):
                hd = hd4 * 4 + q
                p.mm(bk[:, q * 128:(q + 1) * 128], SELS[:, hd % 16, :], ri[:, hd // 16, :], True, True,
                     [SELS, ri], [bk])
            evac(p, hd4, rb[:, hd4 * 4:hd4 * 4 + 4, :], bk[:, :].rearrange("p (q t) -> p q t", t=128), [bk], [rb])
        p.op("dve", lambda e: e.memset(yb[0][:, :], 0.0), [], [yb[0]])
        p.op("dve", lambda e: e.memset(yb[1][:, :], 0.0), [], [yb[1]])
        for j in range(NB):
            gb = g.bank[2 + gn % 2]
            gn += 1
            for gg in range(2):
                p.mm(gb[:, gg * 128:(gg + 1) * 128], BT[:, gg, j * 128:(j + 1) * 128], CTt[:, gg, i * 128:(i + 1) * 128],
                     True, True, [BT, CTt], [gb])
            dirs = [0] if j < i else ([1] if j > i else [0, 1])
            gm = None
            if j == i:
                gm = Gm[i % 2]
                for d in range(2):
                    m = g.trile if d == 0 else g.trige
                    p.tt("dve", gm[:, d, :, :], gb[:, 0:256].rearrange("p (a b) -> p a b", b=128),
                         m.unsqueeze(1).to_broadcast([128, 2, 128]), ALU.mult, [gb, g.CT], [gm])
            for d in dirs:
                sc = 1.0 if d == 0 else -1.0
                for h in range(16):
                    hd = d * 16 + h
                    pt = Pt[n % 4]
                    mt = Mt[n % 4]
                    n += 1
                    if j == i:
                        p.ts("dve", pt[:], rb[:, hd, :], TM[:, j, d, 0, h:h + 1], 0.0, ALU.subtract,
                             ALU.min if d == 0 else ALU.max, [rb, TM], [pt])
                        p.act(pt[:], pt[:], AF.Exp, [pt, TM], [pt], scale=sc, bias=TM[:, j, d, 1, h:h + 1])
                        p.tt("dve", mt[:], pt[:], gm[:, d, h // 8, :], ALU.mult, [pt, gm], [mt])
                    else:
                        p.act(pt[:], rb[:, hd, :], AF.Exp, [rb, TM], [pt], scale=sc, bias=TM[:, j, d, 2, h:h + 1])
                        if (i // g.SB) != (j // g.SB):
                            p.stt(mt[:], pt[:], g.LK[:, 0:1], gb[:, (h // 8) * 128:(h // 8 + 1) * 128], ALU.mult, ALU.mult,
                                  [pt, gb, g.LK], [mt])
                        else:
                            p.tt("dve", mt[:], pt[:], gb[:, (h // 8) * 128:(h // 8 + 1) * 128], ALU.mult, [pt, gb], [mt])
                    p.mm(yb[h // 8][:, (h % 8) * 64:(h % 8 + 1) * 64], mt[:], xsb[:, j, h * 64:(h + 1) * 64], False, False,
                         [mt, xsb], [yb[h // 8]], ms=True, sgc=True)
        p.tt("dve", y[:], xs[:], Dbc[:].rearrange("p a b -> p (a b)"), ALU.mult, [xs, Dbc], [y])
        for hf in range(2):
            p.tt("dve", y[:, hf * 512:(hf + 1) * 512], y[:, hf * 512:(hf + 1) * 512], yb[hf][:, :], ALU.add, [y, yb[hf]], [y])
        p.act(z[:], z[:], AF.Silu, [z], [z])
        p.tt("dve", y[:], y[:], z[:], ALU.mult, [y, z], [y])
        p.stt(junk[:], y[:], 1.0, y[:], ALU.mult, ALU.mult, [y], [junk, s], accum_out=s[:, 0:1])
        p.ts("dve", s[:, 1:2], s[:, 0:1], 1.0 / 1024, EPS, ALU.mult, ALU.add, [s], [s])
        p.act(s[:, 2:3], s[:, 1:2], AF.Sqrt, [s], [s])
        p.op("dve", lambda e: e.reciprocal(out=s[:, 3:4], in_=s[:, 2:3]), [s], [s])
        p.stt(y[:], y[:], s[:, 3:4], NG[:], ALU.mult, ALU.mult, [y, s, NG], [y])
        for k in range(8):
            bk = g.bank[4 + (k // 4)]
            p.tr(bk[:, (k % 4) * 128:(k % 4 + 1) * 128], y[:, k * 128:(k + 1) * 128], g.ident, [y, g.CT], [bk])
            if k % 4 == 3:
                evac(p, k // 4, ob[:, k - 3:k + 1, :], bk[:, :].rearrange("p (a b) -> p a b", b=128), [bk], [ob])
        p.dma("sp", g.OT[0][:, i * 128:(i + 1) * 128].rearrange("(k p) t -> p k t", p=128), ob[:], [ob], [g.OT[0]])
    p.pop()
    p.end_phase()


def range_reduce(p, r, a, tmp):
    rb, rap = r
    ab, aap = a
    tb, tap = tmp
    p.ts("dve", tap, aap, 1.0 / (2 * PI), MAGIC, ALU.mult, ALU.add, [ab], [tb])
    p.ts("dve", tap, tap, MAGIC, None, ALU.subtract, None, [tb], [tb])
    p.stt(rap, tap, -2 * PI, aap, ALU.mult, ALU.add, [tb, ab], [rb])
    p.ts("dve", rap, rap, PI, -PI, ALU.min, ALU.max, [rb], [rb])


def sincos(p, sn, cs, ang, tmp):
    range_reduce(p, ang, ang, tmp)
    p.act(sn[1], ang[1], AF.Sin, [ang[0]], [sn[0]])
    p.act(tmp[1], ang[1], AF.Abs, [ang[0]], [tmp[0]])
    p.act(cs[1], tmp[1], AF.Sin, [tmp[0]], [cs[0]], scale=-1.0, bias=PI / 2)


def s5_phase(g, l):
    p, I = g.p, g.I
    T, NB = g.T, g.NB
    p.begin_phase()
    bk7 = g.bank[7]
    tmp = p.sb([128, 128], F32, 'tmp')
    Bblk = p.sb([128, 8, 2, 512], BF16, 'Bblk')
    S = [[p.sb([128, 128], F32, 'S') for _ in range(2)] for _ in range(4)]
    for kk in range(4):
        for ri in range(2):
            p.op("dve", lambda e: e.memset(S[kk][ri][:], 0.0), [], [S[kk][ri]])
    n = 0
    for ct in range(8):
        for kk in range(4):
            for ri, nm in enumerate(('s5_b_re', 's5_b_im')):
                st = S[kk][ri]
                for gs in range(2):
                    gg = ct * 8 + 2 * kk + gs
                    p.dma("sp", st[gs * 64:(gs + 1) * 64, (2 * kk + gs) * 16:(2 * kk + gs + 1) * 16], I[nm].ap()[l, gg], [], [st])
                bk = g.bank[4 + n % 2]
                n += 1
                p.tr(bk[:, 0:128], st[:], g.ident, [st, g.CT], [bk])
                evac(p, n, Bblk[:, ct, ri, kk * 128:(kk + 1) * 128], bk[:, 0:128], [bk], [Bblk])
    dsk = p.sb([128, 8], F32, 'dsk')
    load_T(g, None, I['s5_d'].ap()[l].rearrange("(k p) -> k p", p=128), 8, tmp, bk7)
    p.cp("dve", dsk[:], bk7[:, 0:8], [bk7], [dsk])
    glb = p.sb([128, 8], F32, 'glb')
    load_T(g, None, I['s5_glu_b'].ap()[l].rearrange("(k p) -> k p", p=128), 8, tmp, bk7)
    p.cp("dve", glb[:], bk7[:, 0:8], [bk7], [glb])
    Cblk = p.sb([128, 32, 2, 128], BF16, 'Cblk')
    S2 = [[p.sb([128, 128], F32, 'S2') for _ in range(2)] for _ in range(4)]
    for kk in range(4):
        for ri in range(2):
            p.op("dve", lambda e: e.memset(S2[kk][ri][:], 0.0), [], [S2[kk][ri]])
    AinT = p.sb([128, 2, 4096], F32, 'AinT')
    Aout = p.sb([128, 2, 32, 128], F32, 'Aout')
    for d in range(2):
        for k in range(32):
            ct, kk = k // 4, k % 4
            for ri, nm in enumerate(('s5_c_re', 's5_c_im')):
                st = S2[kk][ri]
                for gs in range(2):
                    gl = 2 * kk + gs
                    p.dma("sp", st[gl * 16:(gl + 1) * 16, gs * 64:(gs + 1) * 64], I[nm].ap()[l, d, ct * 8 + gl], [], [st])
                bk = g.bank[4 + n % 2]
                n += 1
                p.tr(bk[:, 0:128], st[:], g.ident, [st, g.CT], [bk])
                if ri == 0:
                    p.cp("dve", Cblk[:, k, 0, :], bk[:, 0:128], [bk], [Cblk])
                else:
                    p.ts("dve", Cblk[:, k, 1, :], bk[:, 0:128], -1.0, None, ALU.mult, None, [bk], [Cblk])
        p.push()
        mcol = g.CT[:, 640 + d:641 + d]
        negm = g.CT[:, 642 + d:643 + d]
        mrow = g.CT[:, 644 + 128 * d:772 + 128 * d]
        W = 1024
        tl = [p.sb([128, W], F32, 'tb%d' % i) for i in range(10)]
        ls = p.sb([128, 64], F32, 'ls')
        p.dma("sp", ls[:], bass.AP(I['s5_log_step'], (l * 2 + d) * 64, [[0, 128], [1, 64]]), [], [ls])
        p.act(ls[:], ls[:], AF.Exp, [ls], [ls])
        for q in range(4096 // W):
            ar, ai, al, be, t1, t2, t3, t4, t5, t6 = tl
            o0 = (l * 2 + d) * 4096 + q * W
            p.dma("sp", ar[:], bass.AP(I['s5_a_re'], o0, [[0, 128], [1, W]]), [], [ar])
            p.dma("sp", ai[:], bass.AP(I['s5_a_im'], o0, [[0, 128], [1, W]]), [], [ai])
            stb = ls[:, q * (W // 64):(q + 1) * (W // 64)].unsqueeze(2).to_broadcast([128, W // 64, 64])
            v3 = lambda t: t[:].rearrange("p (a b) -> p a b", b=64)
            p.tt("dve", v3(al), v3(ar), stb, ALU.mult, [ar, ls], [al])
            p.tt("dve", v3(be), v3(ai), stb, ALU.mult, [ai, ls], [be])
            p.act(t1[:], al[:], AF.Exp, [al], [t1])
            p.cp("dve", t6[:], be[:], [be], [t6])
            sincos(p, (t2, t2[:]), (t3, t3[:]), (t6, t6[:]), (t4, t4[:]))
            p.tt("dve", t3[:], t3[:], t1[:], ALU.mult, [t3, t1], [t3])
            p.tt("dve", t2[:], t2[:], t1[:], ALU.mult, [t2, t1], [t2])
            p.ts("dve", t3[:], t3[:], -1.0, None, ALU.add, None, [t3], [t3])
            p.tt("dve", t4[:], ar[:], ar[:], ALU.mult, [ar], [t4])
            p.tt("dve", t1[:], ai[:], ai[:], ALU.mult, [ai], [t1])
            p.tt("dve", t4[:], t4[:], t1[:], ALU.add, [t4, t1], [t4])
            p.op("dve", lambda e: e.reciprocal(out=t4[:], in_=t4[:]), [t4], [t4])
            p.tt("dve", t1[:], t3[:], ar[:], ALU.mult, [t3, ar], [t1])
            p.tt("dve", t5[:], t2[:], ai[:], ALU.mult, [t2, ai], [t5])
            p.tt("dve", t1[:], t1[:], t5[:], ALU.add, [t1, t5], [t1])
            p.tt("dve", t1[:], t1[:], t4[:], ALU.mult, [t1, t4], [t1])
            p.tt("dve", t5[:], t2[:], ar[:], ALU.mult, [t2, ar], [t5])
            p.tt("dve", t6[:], t3[:], ai[:], ALU.mult, [t3, ai], [t6])
            p.tt("dve", t5[:], t5[:], t6[:], ALU.subtract, [t5, t6], [t5])
            p.tt("dve", t5[:], t5[:], t4[:], ALU.mult, [t5, t4], [t5])
            p.act(t4[:], al[:], AF.Exp, [al], [t4], scale=negm)
            p.ts("dve", t6[:], be[:], mcol, None, ALU.mult, None, [be, g.CT], [t6])
            sincos(p, (t2, t2[:]), (t3, t3[:]), (t6, t6[:]), (ar, ar[:]))
            p.tt("dve", t3[:], t3[:], t4[:], ALU.mult, [t3, t4], [t3])
            p.tt("dve", t2[:], t2[:], t4[:], ALU.mult, [t2, t4], [t2])
            p.ts("dve", t2[:], t2[:], -1.0, None, ALU.mult, None, [t2], [t2])
            sl = slice(q * W, (q + 1) * W)
            p.tt("dve", t4[:], t1[:], t3[:], ALU.mult, [t1, t3], [t4])
            p.tt("dve", t6[:], t5[:], t2[:], ALU.mult, [t5, t2], [t6])
            p.tt("dve", AinT[:, 0, sl], t4[:], t6[:], ALU.subtract, [t4, t6], [AinT])
            p.tt("dve", t4[:], t1[:], t2[:], ALU.mult, [t1, t2], [t4])
            p.tt("dve", t6[:], t5[:], t3[:], ALU.mult, [t5, t3], [t6])
            p.tt("dve", AinT[:, 1, sl], t4[:], t6[:], ALU.add, [t4, t6], [AinT])
        asm = p.sb([128, 3, 32], F32, 'asm')
        for qi, nm in enumerate(('s5_a_re', 's5_a_im')):
            load_T(g, None, I[nm].ap()[l, d].rearrange("(k gs) p -> k (gs p)", gs=2), 32, tmp, bk7)
            p.cp("dve", asm[:, qi, :], bk7[:, 0:32], [bk7], [asm])
        l32 = p.sb([32, 2, 64], F32, 'l32')
        l2 = p.sb([32, 2], F32, 'l2')
        p.dma("sp", l2[:], I['s5_log_step'].ap()[l, d].rearrange("(k gs) -> k gs", gs=2), [], [l2])
        p.act(l2[:], l2[:], AF.Exp, [l2], [l2])
        p.cp("dve", l32[:], l2[:].unsqueeze(2).to_broadcast([32, 2, 64]), [l2], [l32])
        p.tr(bk7[:, 0:32], l32[:].rearrange("p a b -> p (a b)"), g.ident[0:32, 0:32], [l32, g.CT], [bk7])
        p.cp("dve", asm[:, 2, :], bk7[:, 0:32], [bk7], [asm])
        p.tt("dve", asm[:, 0, :], asm[:, 0, :], asm[:, 2, :], ALU.mult, [asm], [asm])
        p.tt("dve", asm[:, 1, :], asm[:, 1, :], asm[:, 2, :], ALU.mult, [asm], [asm])
        A3 = lambda t: t[:].rearrange("p (a b) -> p a b", b=128)
        for q in range(4):
            t1, t2, t3, t4, t6 = tl[0], tl[1], tl[2], tl[3], tl[4]
            ks = slice(q * 8, (q + 1) * 8)
            mb = mrow.unsqueeze(1).to_broadcast([128, 8, 128])
            p.tt("dve", A3(t1), mb, asm[:, 0, ks].unsqueeze(2).to_broadcast([128, 8, 128]), ALU.mult, [g.CT, asm], [t1])
            p.act(t1[:], t1[:], AF.Exp, [t1], [t1])
            p.tt("dve", A3(t6), mb, asm[:, 1, ks].unsqueeze(2).to_broadcast([128, 8, 128]), ALU.mult, [g.CT, asm], [t6])
            sincos(p, (t2, t2[:]), (t3, t3[:]), (t6, t6[:]), (t4, t4[:]))
            p.tt("dve", Aout[:, 0, ks, :], A3(t3), A3(t1), ALU.mult, [t3, t1], [Aout])
            p.tt("dve", Aout[:, 1, ks, :], A3(t2), A3(t1), ALU.mult, [t2, t1], [Aout])
        p.pop()
        p.push()
        tri = g.CB[:, 128:256] if d == 0 else g.CB[:, 256:384]
        NCH = 2
        ubs = [p.sb([128, T], BF16, 'ub') for _ in range(NCH)]
        ufc = [p.sb([128, 128], F32, 'ufc') for _ in range(4)]
        SBt = [[p.sb([128, 2, 512], BF16, 'SBt') for _ in range(2)] for _ in range(NCH)]
        w4 = [[p.sb([128, 512], F32, 'w4') for _ in range(4)] for _ in range(NCH)]
        Xf = [[p.sb([128, 2, 128], F32, 'Xf') for _ in range(4)] for _ in range(NCH)]
        Xb = [[p.sb([128, 2, 128], BF16, 'Xb') for _ in range(4)] for _ in range(NCH)]
        x4 = [[[p.sb([128, 128], F32, 'x4') for _ in range(4)] for _ in range(4)] for _ in range(NCH)]
        xp = [[p.sb([128, 2], F32, 'xprev') for _ in range(4)] for _ in range(NCH)]
        yo = [[p.sb([128, 128], F32, 'yo') for _ in range(2)] for _ in range(NCH)]
        yf = [[p.sb([128, 128], F32, 'yf') for _ in range(2)] for _ in range(NCH)]
        gb_ = [[p.sb([128, 128], BF16, 'gb') for _ in range(2)] for _ in range(NCH)]
        n2 = 0
        nu = 0
        order = list(range(NB)) if d == 0 else list(range(NB - 1, -1, -1))
        ccol = 127 if d == 0 else 0
        for cg in range(8 // NCH):
            for sl in range(NCH):
                ct = cg * NCH + sl
                p.dma("pool", ubs[sl][:], g.PFM[1664 + ct * 128:1664 + (ct + 1) * 128, :], [g.PFM], [ubs[sl]])
                for kk in range(4):
                    p.op("dve", lambda e: e.memset(xp[sl][kk][:], 0.0), [], [xp[sl][kk]])
            for c in order:
                cs_ = slice(c * 128, (c + 1) * 128)
                for sl in range(NCH):
                    ct = cg * NCH + sl
                    u_b = ubs[sl]
                    if g.NSEG == 2 and c == (g.SB if d == 0 else g.SB - 1):
                        for kk in range(4):
                            p.ts("dve", xp[sl][kk][:, :], xp[sl][kk][:, :], g.LK[:, 0:1], None, ALU.mult, None,
                                 [xp[sl][kk], g.LK], [xp[sl][kk]])
                    bre = g.bank[sl * 2]
                    bim = g.bank[sl * 2 + 1]
                    sbt = SBt[sl][n2 % 2]
                    y_o = yo[sl][n2 % 2]
                    ww = w4[sl]
                    p.mm(bre[:, :], u_b[:, cs_], Bblk[:, ct, 0, :], True, True, [u_b, Bblk], [bre])
                    p.mm(bim[:, :], u_b[:, cs_], Bblk[:, ct, 1, :], True, True, [u_b, Bblk], [bim])
                    are = AinT[:, 0, ct * 512:(ct + 1) * 512]
                    aim = AinT[:, 1, ct * 512:(ct + 1) * 512]
                    p.tt("dve", ww[0][:], bre[:, :], are, ALU.mult, [bre, AinT], [ww[0]])
                    p.tt("dve", ww[1][:], bim[:, :], aim, ALU.mult, [bim, AinT], [ww[1]])
                    p.tt("pool", sbt[:, 0, :], ww[0][:], ww[1][:], ALU.subtract, [ww[0], ww[1]], [sbt])
                    p.tt("dve", ww[2][:], bre[:, :], aim, ALU.mult, [bre, AinT], [ww[2]])
                    p.tt("dve", ww[3][:], bim[:, :], are, ALU.mult, [bim, AinT], [ww[3]])
                    p.tt("pool", sbt[:, 1, :], ww[2][:], ww[3][:], ALU.add, [ww[2], ww[3]], [sbt])
                    ybk = g.bank[6 + sl]
                    for kk in range(4):
                        k = ct * 4 + kk
                        zb = g.bank[4 + kk % 2]
                        xf = Xf[sl][kk]
                        xb = Xb[sl][kk]
                        xx = x4[sl][kk]
                        xpk = xp[sl][kk]
                        for ri in range(2):
                            p.mm(zb[:, ri * 128:(ri + 1) * 128], sbt[:, ri, kk * 128:(kk + 1) * 128], tri, True, True,
                                 [sbt, g.CB], [zb])
                        zre, zim = zb[:, 0:128], zb[:, 128:256]
                        aor, aoi = Aout[:, 0, k, :], Aout[:, 1, k, :]
                        pre, pim = xpk[:, 0:1], xpk[:, 1:2]
                        p.stt(xx[0][:], zre, pre, aor, ALU.add, ALU.mult, [zb, xpk, Aout], [xx[0]])
                        p.stt(xx[1][:], zim, pim, aoi, ALU.add, ALU.mult, [zb, xpk, Aout], [xx[1]])
                        p.stt(xx[2][:], zre, pre, aoi, ALU.add, ALU.mult, [zb, xpk, Aout], [xx[2]])
                        p.stt(xx[3][:], zim, pim, aor, ALU.add, ALU.mult, [zb, xpk, Aout], [xx[3]])
                        p.tt("pool", xf[:, 0, :], xx[0][:], xx[1][:], ALU.subtract, [xx[0], xx[1]], [xf])
                        p.tt("pool", xf[:, 1, :], xx[2][:], xx[3][:], ALU.add, [xx[2], xx[3]], [xf])
                        p.cp("act", xb[:], xf[:], [xf], [xb])
                        p.cp("act", xpk[:, :], xf[:, :, ccol], [xf], [xpk])
                        for ri in range(2):
                            p.mm(ybk[:, 0:128], Cblk[:, k, ri, :], xb[:, ri, :], kk == 0 and ri == 0, kk == 3 and ri == 1,
                                 [Cblk, xb], [ybk])
                    if d == 0:
                        p.cp("act", y_o[:], ybk[:, 0:128], [ybk], [y_o])
                        p.dma("sp", g.YF[ct * 128:(ct + 1) * 128, cs_], y_o[:], [y_o], [g.YF])
                    else:
                        y_f = yf[sl][n2 % 2]
                        gbf = gb_[sl][n2 % 2]
                        u_f = ufc[nu % 4]
                        nu += 1
                        p.dma("sp", y_f[:], g.YF[ct * 128:(ct + 1) * 128, cs_], [g.YF], [y_f])
                        p.dma("sp", u_f[:], g.PFM[1664 + ct * 128:1664 + (ct + 1) * 128, cs_], [g.PFM], [u_f])
                        p.tt("dve", y_o[:], y_f[:], ybk[:, 0:128], ALU.add, [y_f, ybk], [y_o])
                        p.stt(y_o[:], u_f[:], dsk[:, ct:ct + 1], y_o[:], ALU.mult, ALU.add, [u_f, dsk, y_o], [y_o])
                        p.tt("dve", y_f[:], y_o[:], y_o[:], ALU.mult, [y_o], [y_f])
                        p.ts("dve", y_f[:], y_f[:], 0.044715, 1.0, ALU.mult, ALU.add, [y_f], [y_f])
                        p.tt("dve", y_f[:], y_f[:], y_o[:], ALU.mult, [y_f, y_o], [y_f])
                        p.act(y_f[:], y_f[:], AF.Sigmoid, [y_f], [y_f], scale=1.5957691216057308)
                        p.tt("dve", y_o[:], y_o[:], y_f[:], ALU.mult, [y_o, y_f], [y_o])
                        p.cp("act", gbf[:], y_o[:], [y_o], [gbf])
                        p.dma("sp", g.YF[ct * 128:(ct + 1) * 128, cs_], y_o[:], [y_o], [g.YF])
                        p.dma("sp", g.QT[ct * 128:(ct + 1) * 128, cs_], gbf[:], [gbf], [g.QT])
                n2 += 1
        p.pop()
    p.push()
    ws = WStream(p)
    ATs = [p.sb([128, 8, 512], BF16, 'AT') for _ in range(2)]
    gfs = [p.sb([128, 512], F32, 'gf') for _ in range(2)]
    sgs = [p.sb([128, 512], F32, 'sg') for _ in range(2)]
    obs = [p.sb([128, 512], BF16, 'ob') for _ in range(2)]
    Wg = I['s5_glu_w'].ap()[l]
    n = 0
    bi = 0
    for s in range(g.NS):
        AT = ATs[s % 2]
        load_AT(g, AT, g.QT, 8, s)
        for cc in range(2):
            banks = g.bank[bi * 4:(bi + 1) * 4]
            bi ^= 1
            gemm_chunk(g, ws, AT, 8, Wg, cc * 512, 512, 'fm', banks)
            for ct in range(4):
                r0 = cc * 512 + ct * 128
                gf, sg, ob = gfs[n % 2], sgs[n % 2], obs[n % 2]
                n += 1
                p.dma("sp", gf[:], g.YF[r0:r0 + 128, s * 512:(s + 1) * 512], [g.YF], [gf])
                p.act(sg[:], banks[ct][:, :], AF.Sigmoid, [banks[ct], glb], [sg], bias=glb[:, cc * 4 + ct:cc * 4 + ct + 1])
                p.tt("dve", ob[:], gf[:], sg[:], ALU.mult, [gf, sg], [ob])
                p.dma("sp", g.OT[2][r0:r0 + 128, s * 512:(s + 1) * 512], ob[:], [ob], [g.OT[2]])
    p.pop()
    p.end_phase()


def layer(g, l, src, dst, stages):
    p = g.p
    mts = g.modT[l]
    norm_phase(g, src, g.G1[l], mts, 0)
    inproj_phase(g, l)
    ssd_phase(g, l)
    swa_phase(g, l)
    s5_phase(g, l)
    na_phase(g, l)
    if g.dbg and l == 0:
        for b in range(4):
            p.dma("sp", g.dbg['ot%d' % b][:, :], g.OT[b][:, :], [g.OT[b]], [g.dbg['ot%d' % b]])
    merge_phase(g, l)
    resid_gemm_phase(g, g.MT, 32, g.I['w_out'].ap()[l], [(mt, mt, 64) for mt in mts], src, g.X1, g.WC['w_out'])
    if g.dbg and l == 0:
        p.dma("sp", g.dbg['x1'][:, :], g.X1[:, :], [g.X1], [g.dbg['x1']])
    norm_phase(g, g.X1, g.G2[l], mts, 3)
    ffn1_phase(g, l)
    resid_gemm_phase(g, g.UT, 86, g.I['ffn_w2'].ap()[l], [(mt, mt, 160) for mt in mts], g.X1, dst, g.WC['w2'])


_CACHE = {}


def kernel(**inputs):
    w = {n: np.ascontiguousarray(np.asarray(inputs[n], dtype=np.float32)) for n in WSHAPES}
    xp = np.asarray(inputs['x_prompt'], dtype=np.float32)
    xs = np.asarray(inputs['x_sample'], dtype=np.float32)
    cp_ = np.asarray(inputs['c_prompt'], dtype=np.float32)
    cs = np.asarray(inputs['c_sample'], dtype=np.float32)
    T = xp.shape[1]
    nc = build(T, 2, NSEG=2)
    consts = make_consts()
    rope = make_rope(T)
    maps = []
    for b in range(2):
        lk = np.zeros((128, 2), np.float32)
        lk[:, 0] = 1.0
        m = {'x': np.ascontiguousarray(xp[b]), 'c': np.ascontiguousarray(np.stack([cp_[b], cp_[b]]).reshape(64, 128)),
             'link': lk, 'consts': consts, 'rope': rope}
        m.update(w)
        maps.append(m)
    for b in range(2):
        lk = np.zeros((128, 2), np.float32)
        lk[:, 1] = 1.0
        m = {'x': np.ascontiguousarray(xs[2 * b:2 * b + 2].reshape(T, D)),
             'c': np.ascontiguousarray(cs[2 * b:2 * b + 2].reshape(64, 128)), 'link': lk, 'consts': consts, 'rope': rope}
        m.update(w)
        maps.append(m)
    res = run_bass_kernel_spmd(nc, maps, core_ids=list(range(4))).results
    yp = np.stack([res[0]['y'], res[1]['y']], axis=0).astype(np.float32)
    ys = np.concatenate([res[2]['y'].reshape(2, T // 2, D), res[3]['y'].reshape(2, T // 2, D)], axis=0).astype(np.float32)
    return (yp, ys)
```
